# Optimizing a Trainium2 kernel written in Bass

```python
import math
import jax, jax.numpy as jnp
from jax import lax
import numpy as np

D_MODEL = 1024
BATCH = 4
SEQ = 8192
DEPTH = 2

N_MIXERS = 2
EPS = 1e-6
ATTN_HEADS = 16
ATTN_KV_HEADS = 4
ATTN_HEAD_DIM = D_MODEL // ATTN_HEADS
ATTN_GROUP = ATTN_HEADS // ATTN_KV_HEADS
WINDOW = 128
BLOCK = 128
ROPE_THETA = 500000.0
ROPE_DIM = ATTN_HEAD_DIM // 4
Q_W = ATTN_HEADS * ATTN_HEAD_DIM
KV_W = ATTN_KV_HEADS * ATTN_HEAD_DIM
MEM_LEN = 256
MEM_HEADS = 4
MEM_HEAD_DIM = 128
MEM_W = MEM_HEADS * MEM_HEAD_DIM
LRU_WIDTH = D_MODEL
LRU_BLOCKS = 8
LRU_BLOCK_DIM = LRU_WIDTH // LRU_BLOCKS
LRU_C = 8.0
CONV_WIDTH = 4
CONV_LEFT = (CONV_WIDTH - 1) // 2
ATTN_IN_W = Q_W + 2 * KV_W + MEM_W
LRU_IN_W = 2 * LRU_WIDTH + MEM_W
MIX_OUT_W = Q_W + MEM_W
D_FF = 4 * D_MODEL
NEG = -1e30

kernel_name = "hybrid_window_gqa_rglru_memxattn_encoder"


def rms_norm(x, g):
    xf = x.astype(jnp.float32)
    y = xf * lax.rsqrt(jnp.mean(xf * xf, axis=-1, keepdims=True) + EPS) * g.astype(jnp.float32)
    return y.astype(x.dtype)


def partial_rotary(t, positions):
    half = ROPE_DIM // 2
    inv_freq = ROPE_THETA ** (-2.0 * jnp.arange(half, dtype=jnp.float32) / ROPE_DIM)
    ang = positions.astype(jnp.float32)[..., None] * inv_freq
    cos = jnp.cos(ang)[:, :, None, :]
    sin = jnp.sin(ang)[:, :, None, :]
    tr = t[..., :ROPE_DIM].astype(jnp.float32)
    t1, t2 = tr[..., :half], tr[..., half:]
    rot = jnp.concatenate([t1 * cos - t2 * sin, t2 * cos + t1 * sin], axis=-1)
    return jnp.concatenate([rot.astype(t.dtype), t[..., ROPE_DIM:]], axis=-1)


def window_gqa(q, k, v, sinks):
    B, S = q.shape[0], q.shape[1]
    nb = S // BLOCK
    qb = q.reshape(B, nb, BLOCK, ATTN_KV_HEADS, ATTN_GROUP, ATTN_HEAD_DIM)
    pad = ((0, 0), (BLOCK, BLOCK), (0, 0), (0, 0))

    def bands(t):
        tb = jnp.pad(t, pad).reshape(B, nb + 2, BLOCK, ATTN_KV_HEADS, ATTN_HEAD_DIM)
        return jnp.concatenate([tb[:, :nb], tb[:, 1:nb + 1], tb[:, 2:]], axis=2)

    kb, vb = bands(k), bands(v)
    scores = jnp.einsum('bnqhgd,bnkhd->bnhgqk', qb, kb,
                        preferred_element_type=jnp.float32) * (ATTN_HEAD_DIM ** -0.5)
    q_idx = jnp.arange(BLOCK)
    k_idx = jnp.arange(3 * BLOCK)
    rel = k_idx[None, :] - BLOCK - q_idx[:, None]
    k_abs = jnp.arange(nb)[:, None] * BLOCK - BLOCK + k_idx[None, :]
    mask = (jnp.abs(rel) <= WINDOW)[None] & ((k_abs >= 0) & (k_abs < S))[:, None, :]
    scores = jnp.where(mask[None, :, None, None], scores, NEG)
    s = sinks.astype(jnp.float32).reshape(ATTN_KV_HEADS, ATTN_GROUP)[None, None, :, :, None, None]
    m = jnp.maximum(jnp.max(scores, axis=-1, keepdims=True), s)
    p = jnp.exp(scores - m)
    probs = p / (jnp.sum(p, axis=-1, keepdims=True) + jnp.exp(s - m))
    out = jnp.einsum('bnhgqk,bnkhd->bnqhgd', probs.astype(v.dtype), vb)
    return out.reshape(B, S, Q_W)


def memory_attention(mq, mk, mv):
    B, S = mq.shape[0], mq.shape[1]
    sc = jnp.einsum('bshd,bmhd->bhsm', mq, mk,
                    preferred_element_type=jnp.float32) * (MEM_HEAD_DIM ** -0.5)
    p = jax.nn.softmax(sc, axis=-1)
    out = jnp.einsum('bhsm,bmhd->bshd', p.astype(mv.dtype), mv)
    return out.reshape(B, S, MEM_W)


def centred_depthwise_conv(x, w, b):
    S = x.shape[1]
    xp = jnp.pad(x, ((0, 0), (CONV_LEFT, CONV_WIDTH - 1 - CONV_LEFT), (0, 0)))
    y = b
    for tap in range(CONV_WIDTH):
        y = y + xp[:, tap:tap + S] * w[tap]
    return y


def block_diag_linear(x, w, b):
    B, S = x.shape[0], x.shape[1]
    xr = x.reshape(B, S, LRU_BLOCKS, LRU_BLOCK_DIM)
    return jnp.einsum('bsnd,nde->bsne', xr, w).reshape(B, S, LRU_WIDTH) + b


def _linear_combine(c1, c2):
    a1, b1 = c1
    a2, b2 = c2
    return a1 * a2, a2 * b1 + b2


def rg_lru(x, wa, ba, wx, bx, lam, reverse):
    xf = x.astype(jnp.float32)
    r = jax.nn.sigmoid(block_diag_linear(x, wa, ba).astype(jnp.float32))
    i = jax.nn.sigmoid(block_diag_linear(x, wx, bx).astype(jnp.float32))
    log_a = -LRU_C * r * jax.nn.softplus(-lam.astype(jnp.float32))
    a = jnp.exp(log_a)
    u = jnp.sqrt(-jnp.expm1(2.0 * log_a)) * (i * xf)
    _, h = lax.associative_scan(_linear_combine, (a, u), axis=1, reverse=reverse)
    return h.astype(x.dtype)


def attn_mixer(h, positions, w_in, sinks):
    B, S = h.shape[0], h.shape[1]
    p = h @ w_in
    q, k, v, mq = jnp.split(p, [Q_W, Q_W + KV_W, Q_W + 2 * KV_W], axis=-1)
    q = partial_rotary(q.reshape(B, S, ATTN_HEADS, ATTN_HEAD_DIM), positions)
    k = partial_rotary(k.reshape(B, S, ATTN_KV_HEADS, ATTN_HEAD_DIM), positions)
    v = v.reshape(B, S, ATTN_KV_HEADS, ATTN_HEAD_DIM)
    return window_gqa(q, k, v, sinks), mq.reshape(B, S, MEM_HEADS, MEM_HEAD_DIM)


def lru_mixer(h, w_in, conv_w, conv_b, wa, ba, wx, bx, lam):
    B, S = h.shape[0], h.shape[1]
    p = h @ w_in
    xb, gate, mq = jnp.split(p, [LRU_WIDTH, 2 * LRU_WIDTH], axis=-1)
    xc = centred_depthwise_conv(xb, conv_w, conv_b)
    y = (rg_lru(xc, wa[0], ba[0], wx[0], bx[0], lam[0], False)
         + rg_lru(xc, wa[1], ba[1], wx[1], bx[1], lam[1], True))
    y = y * jax.nn.gelu(gate)
    return y, mq.reshape(B, S, MEM_HEADS, MEM_HEAD_DIM)


def squared_relu_mlp(h, w_up, w_down):
    return jnp.square(jax.nn.relu(h @ w_up)) @ w_down


def setup_inputs(seed: int = 0) -> dict:
    key = jax.random.key(seed)
    ks = jax.random.split(key, 24)
    n_attn = (DEPTH + N_MIXERS - 1) // N_MIXERS
    n_lru = DEPTH // N_MIXERS
    f32 = jnp.float32

    def nrm(k, shape, scale):
        return jax.random.normal(k, shape, f32) * scale

    u = jax.random.uniform(ks[20], (n_lru, 2, LRU_WIDTH), f32, minval=0.9, maxval=0.999)
    s = u ** (1.0 / LRU_C)
    lam = jnp.log(s) - jnp.log1p(-s)
    positions = (jnp.arange(SEQ, dtype=jnp.int32)[None, :]
                 + jax.random.randint(ks[21], (BATCH, 1), 0, 1024, dtype=jnp.int32))
    return {
        "x": nrm(ks[0], (BATCH, SEQ, D_MODEL), 1.0),
        "mem": nrm(ks[1], (BATCH, MEM_LEN, D_MODEL), 1.0),
        "positions": positions,
        "mix_norm": 1.0 + nrm(ks[2], (DEPTH, D_MODEL), 0.05),
        "mlp_norm": 1.0 + nrm(ks[3], (DEPTH, D_MODEL), 0.05),
        "mem_norm": 1.0 + nrm(ks[4], (D_MODEL,), 0.05),
        "final_norm": 1.0 + nrm(ks[5], (D_MODEL,), 0.05),
        "w_mem_kv": nrm(ks[6], (DEPTH, D_MODEL, 2 * MEM_W), D_MODEL ** -0.5),
        "w_out": nrm(ks[7], (DEPTH, MIX_OUT_W, D_MODEL), MIX_OUT_W ** -0.5),
        "w_up": nrm(ks[8], (DEPTH, D_MODEL, D_FF), D_MODEL ** -0.5),
        "w_down": nrm(ks[9], (DEPTH, D_FF, D_MODEL), D_FF ** -0.5),
        "attn_w_in": nrm(ks[10], (n_attn, D_MODEL, ATTN_IN_W), D_MODEL ** -0.5),
        "attn_sinks": nrm(ks[11], (n_attn, ATTN_HEADS), 0.5),
        "lru_w_in": nrm(ks[12], (n_lru, D_MODEL, LRU_IN_W), D_MODEL ** -0.5),
        "lru_conv_w": nrm(ks[13], (n_lru, CONV_WIDTH, LRU_WIDTH), CONV_WIDTH ** -0.5),
        "lru_conv_b": nrm(ks[14], (n_lru, LRU_WIDTH), 0.02),
        "lru_wa": nrm(ks[15], (n_lru, 2, LRU_BLOCKS, LRU_BLOCK_DIM, LRU_BLOCK_DIM), LRU_BLOCK_DIM ** -0.5),
        "lru_ba": nrm(ks[16], (n_lru, 2, LRU_WIDTH), 0.02),
        "lru_wx": nrm(ks[17], (n_lru, 2, LRU_BLOCKS, LRU_BLOCK_DIM, LRU_BLOCK_DIM), LRU_BLOCK_DIM ** -0.5),
        "lru_bx": nrm(ks[18], (n_lru, 2, LRU_WIDTH), 0.02),
        "lru_lambda": lam,
    }


def reference(x, mem, positions, mix_norm, mlp_norm, mem_norm, final_norm, w_mem_kv, w_out,
              w_up, w_down, attn_w_in, attn_sinks, lru_w_in, lru_conv_w, lru_conv_b,
              lru_wa, lru_ba, lru_wx, lru_bx, lru_lambda):
    B = mem.shape[0]
    mem_n = rms_norm(mem, mem_norm)
    for l in range(DEPTH):
        kind = l % N_MIXERS
        j = l // N_MIXERS
        h = rms_norm(x, mix_norm[l])
        kv = mem_n @ w_mem_kv[l]
        mk = kv[..., :MEM_W].reshape(B, MEM_LEN, MEM_HEADS, MEM_HEAD_DIM)
        mv = kv[..., MEM_W:].reshape(B, MEM_LEN, MEM_HEADS, MEM_HEAD_DIM)
        if kind == 0:
            mixed, mq = attn_mixer(h, positions, attn_w_in[j], attn_sinks[j])
        else:
            mixed, mq = lru_mixer(h, lru_w_in[j], lru_conv_w[j], lru_conv_b[j], lru_wa[j],
                                  lru_ba[j], lru_wx[j], lru_bx[j], lru_lambda[j])
        mo = memory_attention(mq, mk, mv)
        x = x + jnp.concatenate([mixed, mo], axis=-1) @ w_out[l]
        x = x + squared_relu_mlp(rms_norm(x, mlp_norm[l]), w_up[l], w_down[l])
    return rms_norm(x, final_norm)
```

```python
import contextlib
import math
import numpy as np
import concourse.bass as bass
import concourse.mybir as mybir
from concourse.ap import AP
from concourse.bass_utils import run_bass_kernel_spmd

F32 = mybir.dt.float32
BF16 = mybir.dt.bfloat16
I32 = mybir.dt.int32
AF = mybir.ActivationFunctionType
ALU = mybir.AluOpType
AX = mybir.AxisListType

D = 1024
KC = 8
NT = 512
DFF = 4096
MEM = 256
EPS = 1e-6
NCORES = 4
NTOK = 8192
NBLK = NTOK // 128
NTILE = NTOK // NT
RING = 5
NSLOT = 12
SEM_LIMIT = 30000
SPREAD8 = True
CLAMP_C = float(np.float32(1.0) - np.float32(2.0 ** -24))

V_MIXG0, V_MIXG1, V_MLPG0, V_MLPG1, V_MEMG, V_FING, V_CONVB = 0, 1, 2, 3, 4, 5, 6
V_CONVW = 7
V_BA = 11
V_BX = 13
V_LAM = 15
NVEC = 17


def mk(name, *args, **kw):
    return lambda e: getattr(e, name)(*args, **kw)


def bc_mid(ap2, n):
    a = ap2.ap
    return AP(ap2.tensor, ap2.offset, [list(a[0]), [0, n], list(a[1])])


def bc_last(ap2, n):
    a = ap2.ap
    return AP(ap2.tensor, ap2.offset, [list(a[0]), list(a[1]), [0, n]])


class Buf:
    __slots__ = ("name", "w", "r")
    ALL = []

    def __init__(self, name):
        self.name = name
        self.w = None
        self.r = []
        Buf.ALL.append(self)


class Prog:
    CE = ("pe", "act", "dve", "pool")
    ENG = ("pe", "act", "dve", "pool", "sp")

    def __init__(self, n_dma_sems=30):
        self.q = {e: [] for e in self.ENG}
        self.cnt = {e: 0 for e in self.CE}
        self.epoch = {e: 0 for e in self.CE}
        self.seen = {e: {} for e in self.ENG}
        self.n_dma = n_dma_sems
        self.dma_val = [0] * n_dma_sems
        self.dma_rr = 0
        self.n_g = 0
        self.nops = 0

    def _next_ev(self, eng):
        if self.cnt[eng] >= SEM_LIMIT:
            return ((eng, self.epoch[eng] + 1), 1)
        return ((eng, self.epoch[eng]), self.cnt[eng] + 1)

    def _commit(self, eng):
        if self.cnt[eng] >= SEM_LIMIT:
            self.epoch[eng] += 1
            self.cnt[eng] = 0
        self.cnt[eng] += 1
        return ((eng, self.epoch[eng]), self.cnt[eng])

    def _deps(self, eng, reads, writes):
        need = {}

        def add(ev):
            if ev is None:
                return
            k, v = ev
            if need.get(k, 0) < v:
                need[k] = v
        for b in reads:
            add(b.w)
        for b in writes:
            add(b.w)
            for ev in b.r:
                add(ev)
        waits = []
        seen = self.seen[eng]
        for k, v in need.items():
            if eng == "pe" and k[0] == "pe":
                continue
            if seen.get(k, 0) >= v:
                continue
            seen[k] = v
            waits.append((k, v))
        return waits

    def _mark(self, ev, reads, writes):
        for b in reads:
            b.r.append(ev)
            if len(b.r) > 64:
                m = {}
                for k, v in b.r:
                    if m.get(k, 0) < v:
                        m[k] = v
                b.r = list(m.items())
        for b in writes:
            b.w = ev
            b.r = []

    def op(self, eng, fn, reads=(), writes=(), inc=True):
        waits = self._deps(eng, reads, writes)
        if inc:
            ev = self._commit(eng)
            self.q[eng].append((waits, fn, ev))
        else:
            ev = self._next_ev(eng)
            self.q[eng].append((waits, fn, None))
        self._mark(ev, reads, writes)
        self.nops += 1
        return ev

    def dma(self, fn, reads=(), writes=(), eng="sp"):
        if eng == "pool":
            key = ("g", self.n_g)
            self.n_g += 1
            waits = self._deps(eng, reads, writes)
            ev = (key, 16)
            self.q[eng].append((waits, fn, ev))
            self._mark(ev, reads, writes)
            self.nops += 1
            return ev
        s = self.dma_rr
        self.dma_rr = (self.dma_rr + 1) % self.n_dma
        key = ("d", s)
        waits = self._deps(eng, reads, writes)
        pv = self.dma_val[s]
        if pv > 0 and self.seen[eng].get(key, 0) < pv:
            self.seen[eng][key] = pv
            waits.append((key, pv))
        self.dma_val[s] += 16
        ev = (key, self.dma_val[s])
        self.q[eng].append((waits, fn, ev))
        self._mark(ev, reads, writes)
        self.nops += 1
        return ev

    def barrier(self):
        for e in self.ENG:
            waits = []
            for f in self.CE:
                if f == e or self.cnt[f] == 0:
                    continue
                k, v = (f, self.epoch[f]), self.cnt[f]
                if self.seen[e].get(k, 0) < v:
                    self.seen[e][k] = v
                    waits.append((k, v))
            for s in range(self.n_dma):
                k, v = ("d", s), self.dma_val[s]
                if v > 0 and self.seen[e].get(k, 0) < v:
                    self.seen[e][k] = v
                    waits.append((k, v))
            for s in range(self.n_g):
                k, v = ("g", s), 16
                if self.seen[e].get(k, 0) < v:
                    self.seen[e][k] = v
                    waits.append((k, v))
            if waits:
                self.q[e].append((waits, None, None))

    def emit(self, nc):
        with contextlib.ExitStack() as st:
            sems = {}
            for e in self.CE:
                for ep in range(self.epoch[e] + 1):
                    sems[(e, ep)] = st.enter_context(nc.semaphore("s_%s%d" % (e, ep)))
            for i in range(self.n_dma):
                sems[("d", i)] = st.enter_context(nc.semaphore("s_d%d" % i))
            for i in range(self.n_g):
                sems[("g", i)] = st.enter_context(nc.semaphore("s_g%d" % i))
            block = st.enter_context(nc.Block())
            prog = self

            def run(engname, engobj):
                for waits, fn, ev in prog.q[engname]:
                    for k, v in waits:
                        engobj.wait_ge(sems[k], v)
                    if fn is None:
                        continue
                    ins = fn(engobj)
                    if ev is not None:
                        k, v = ev
                        ins.then_inc(sems[k], 16 if k[0] in ("d", "g") else 1)

            @block.tensor
            def _(e):
                run("pe", e)

            @block.scalar
            def _(e):
                run("act", e)

            @block.vector
            def _(e):
                run("dve", e)

            @block.gpsimd
            def _(e):
                run("pool", e)

            @block.sync
            def _(e):
                run("sp", e)


class TB:
    def __init__(self, t, bufs):
        self.t = t
        self.bufs = bufs
        self.b = bufs[0]

    def __getitem__(self, k):
        return self.t[k]


def weight_plan():
    sl = {}
    off = [0]

    def add(key, src, k0, kcs, c0, ncs, lsel=None):
        sl[key] = dict(src=src, k0=k0, kcs=kcs, c0=c0, ncs=ncs, off=off[0], lsel=lsel)
        off[0] += kcs * ncs

    add(("kv0", 0), "w_in0", 0, 8, 1024, 512)
    for s in range(2):
        add(("q0", s), "w_in0", 0, 8, 512 * s, 512)
    add(("mq0", 0), "w_in0", 0, 8, 1536, 512)
    for s in range(2):
        add(("xb1", s), "w_in1", 0, 8, 512 * s, 512)
    for s in range(2):
        add(("gate1", s), "w_in1", 0, 8, 1024 + 512 * s, 512)
    add(("mq1", 0), "w_in1", 0, 8, 2048, 512)
    for l in range(2):
        for s in range(2):
            add(("mkv", l, s), "w_mem_kv", 0, 8, 512 * s, 512, lsel=l)
        for s in range(4):
            add(("wo", l, s), "w_out", 0, 12, 256 * s, 256, lsel=l)
        for s in range(8):
            add(("wu", l, s), "w_up", 0, 8, 512 * s, 512, lsel=l)
        for s in range(8):
            add(("wd", l, s), "w_down", 0, 32, 128 * s, 128, lsel=l)
    return sl, off[0]


class WRing:
    def __init__(self, P, slots, wscr, wscr_bufs, plan, seq, need_cast):
        self.P = P
        self.slots = slots
        self.wscr = wscr
        self.wb = wscr_bufs
        self.plan = plan
        self.seq = seq
        self.rec = []
        self.pos = 0
        self.issued = 0
        self.R = len(slots)
        self.need_cast = need_cast

    def get(self, key):
        k = self.pos
        self.pos += 1
        self.rec.append(key)
        if self.seq is None:
            return self.slots[k % self.R]
        assert self.seq[k] == key, (self.seq[k], key)
        upto = min(len(self.seq), k + self.R)
        while self.issued < upto:
            j = self.issued
            kk = self.seq[j]
            self.need_cast(kk)
            d = self.plan[kk]
            n = d["kcs"] * d["ncs"]
            slot = self.slots[j % self.R]
            self.P.dma(mk("dma_start", out=slot.t[:, 0:n], in_=self.wscr[:, d["off"]:d["off"] + n]),
                       reads=[self.wb[kk]], writes=[slot.b])
            self.issued += 1
        return self.slots[k % self.R]


def build_program():
    nc = bass.Bass("TRN2", target_bir_lowering=False)
    plan, wtot = weight_plan()

    def din(name, shape, dt=F32):
        return nc.dram_tensor(name, shape, dt, kind="ExternalInput").ap()

    xT = din("xT", [D, NTOK])
    memT = din("memT", [D, MEM])
    pos = din("pos", [1, NTOK], I32)
    vecs_d = din("vecs", [128, NVEC, 8])
    cv_d = din("cv", [128, 2])
    tri_d = din("tri", [128, 2, 128])
    sinks_d = din("sinks", [1, 16])
    cm_d = din("cm", [2, 2])
    wsrc = {
        "w_in0": din("w_in0", [D, 2048]),
        "w_in1": din("w_in1", [D, 2560]),
        "w_out": din("w_out", [2, 1536, D]),
        "w_up": din("w_up", [2, D, DFF]),
        "w_down": din("w_down", [2, DFF, D]),
        "w_mem_kv": din("w_mem_kv", [2, D, D]),
    }
    wa_d = din("lru_wa", [2, 8, 128, 128])
    wx_d = din("lru_wx", [2, 8, 128, 128])
    outT = nc.dram_tensor("outT", [D, NTOK], F32, kind="ExternalOutput").ap()

    def dscr(name, shape, dt):
        return nc.dram_tensor(name, shape, dt, kind="Internal").ap()

    wscr = dscr("wscr", [128, wtot], BF16)
    hs = dscr("hs", [128, KC, NTOK], BF16)
    hs1 = dscr("hs1", [128, KC, NTOK], BF16)
    kts = dscr("kts", [128, 4, NTOK], BF16)
    vas = dscr("vas", [128, NBLK, 512], BF16)
    x1s = dscr("x1s", [128, KC, NTOK], F32)
    xbs = dscr("xbs", [128, KC, NTOK + 4], F32)
    xcs = dscr("xcs", [128, KC, NTOK], F32)
    h1s = dscr("h1s", [128, KC, NTOK], F32)

    xT3 = xT.rearrange("(c p) t -> p c t", p=128)
    outT3 = outT.rearrange("(c p) t -> p c t", p=128)
    memT3 = memT.rearrange("(c p) t -> p c t", p=128)

    st = contextlib.ExitStack()
    with st:
        def sb(name, shape, dt):
            return st.enter_context(nc.sbuf_tensor("S_" + name, shape, dt))

        def tb(name, shape, dt, nb=1):
            return TB(sb(name, shape, dt), [Buf("%s.%d" % (name, i)) for i in range(nb)])

        xt = tb("xt", [128, KC, NT], F32, KC)
        hb2 = [tb("hb%d" % i, [128, KC, NT], BF16) for i in range(2)]
        hid = tb("hid", [128, 32, NT], BF16, 32)
        hidF = hid.t[:].rearrange("p a b -> p (a b)").bitcast(F32).rearrange("p (a b) -> p a b", b=NT)
        sqb = TB(None, hid.bufs[0:16])
        sqb_ap = hidF[:, 0:8, :]
        xalt = TB(None, hid.bufs[16:32])
        xalt_ap = hidF[:, 8:16, :]
        ring = [tb("ring%d" % i, [128, 4096], BF16) for i in range(RING)]
        ot = tb("ot", [128, KC, NT], BF16, KC)
        mqt = tb("mqt", [128, 4, NT], BF16, 4)
        mot = tb("mot", [128, 4, NT], BF16, 4)
        ones_f = tb("ones_f", [128, 128], F32)
        onesb = tb("onesb", [128, 128], BF16)
        vecs = tb("vecs", [128, NVEC, 8], F32)
        mkt = [tb("mkt%d" % l, [128, 4, MEM], BF16) for l in range(2)]
        mvt = [tb("mvt%d" % l, [128, 2, 512], BF16) for l in range(2)]
        normp = tb("normp", [128, NT], F32)
        normr = tb("normr", [128, NT], F32)
        psum = [TB(st.enter_context(nc.psum_tensor("ps%d" % i, [128, 512], F32)), [Buf("ps%d" % i)]) for i in range(8)]

        ctr = {"ps": 0}
        cur = {}

        def slot():
            return cur["slot"]()

        def ptb():
            return cur["pt"]()

        def bank():
            s = psum[ctr["ps"] % 8]
            ctr["ps"] += 1
            return s

        def rr(lst):
            c = [0]

            def f():
                s = lst[c[0] % len(lst)]
                c[0] += 1
                return s
            return f

        xT_b = Buf("xT")
        hs_b = [Buf("hs%d" % i) for i in range(NTILE)]
        hs1_b = [Buf("hs1_%d" % i) for i in range(NTILE)]
        kts_b = [Buf("kts%d" % i) for i in range(NTILE)]
        vas_b = [Buf("vas%d" % i) for i in range(NTILE)]
        x1s_b = [Buf("x1s%d" % i) for i in range(NTILE)]
        xbs_b = [Buf("xbs%d" % i) for i in range(NTILE)]
        xbs_pad = Buf("xbspad")
        xcs_b = [Buf("xcs%d" % i) for i in range(NTILE)]
        h1s_b = [Buf("h1s%d" % i) for i in range(NTILE)]
        out_b = Buf("out")
        wscr_b = {k: Buf("w" + str(k)) for k in plan}
        const_b = Buf("constin")

        def record(P, W, tag):
            ctr["ps"] = 0
            cast_done = set()
            cast_chain = [Buf("castchain0"), Buf("castchain1")]

            def need_cast(key):
                if key in cast_done:
                    return
                cast_done.add(key)
                d = plan[key]
                src = wsrc[d["src"]]
                if d["lsel"] is not None:
                    src = src[d["lsel"]]
                src3 = src.rearrange("(k p) n -> p k n", p=128)
                kcs, ncs = d["kcs"], d["ncs"]
                n = kcs * ncs
                P.dma(mk("dma_start", out=wscr[:, d["off"]:d["off"] + n].rearrange("p (k n) -> p k n", n=ncs),
                         in_=src3[:, d["k0"]:d["k0"] + kcs, d["c0"]:d["c0"] + ncs]),
                      reads=[const_b], writes=[wscr_b[key], cast_chain[len(cast_done) % 2]], eng="pool")
            W.need_cast = need_cast
            cast_pos = [0]

            def emit_casts(n):
                if W.seq is None:
                    return
                while n > 0 and cast_pos[0] < len(W.seq):
                    k = W.seq[cast_pos[0]]
                    cast_pos[0] += 1
                    if k not in cast_done:
                        need_cast(k)
                        n -= 1

            def mm_group(out_ap, bk, pairs, reads):
                n = len(pairs)
                for i, (l, r) in enumerate(pairs):
                    P.op("pe", mk("matmul", out_ap, lhsT=l, rhs=r, start=(i == 0), stop=(i == n - 1)),
                         reads=reads, writes=[bk.b], inc=(i == n - 1))

            def norm_part1(x_ap, xbufs, nt):
                P.op("act", mk("activation", out=sqb_ap[:, :, 0:nt], in_=x_ap, func=AF.Square),
                     reads=xbufs, writes=sqb.bufs)
                P.op("dve", mk("tensor_reduce", out=normp.t[:, 0:nt],
                               in_=sqb_ap[:, :, 0:nt].rearrange("p c t -> p t c"), axis=AX.X, op=ALU.add),
                     reads=sqb.bufs, writes=[normp.b])

            def norm_part2(nt):
                bk = bank()
                P.op("pe", mk("matmul", bk.t[:, 0:nt], lhsT=ones_f.t[:], rhs=normp.t[:, 0:nt], start=True, stop=True),
                     reads=[normp.b, ones_f.b], writes=[bk.b])
                P.op("act", mk("activation", out=normr.t[:, 0:nt], in_=bk.t[:, 0:nt], func=AF.Sqrt,
                               scale=1.0 / D, bias=EPS), reads=[bk.b], writes=[normr.b])
                P.op("dve", mk("reciprocal", out=normr.t[:, 0:nt], in_=normr.t[:, 0:nt]), reads=[normr.b], writes=[normr.b])
                return normr

            def norm_stats(x_ap, xbufs, nt):
                norm_part1(x_ap, xbufs, nt)
                return norm_part2(nt)

            def apply_norm(h_tb, x_ap, xbufs, rs, nt, gidx):
                for c in range(KC):
                    P.op("dve", mk("scalar_tensor_tensor", out=h_tb.t[:, c, 0:nt], in0=x_ap[:, c, :],
                                   scalar=vecs.t[:, gidx, c:c + 1], in1=rs.t[:, 0:nt], op0=ALU.mult, op1=ALU.mult),
                         reads=[xbufs[c], rs.b, vecs.b], writes=h_tb.bufs)

            P.op("dve", mk("memset", ones_f.t[:], 1.0), writes=[ones_f.b])
            P.op("dve", mk("memset", onesb.t[:], 1.0), writes=[onesb.b])
            P.dma(mk("dma_start", out=vecs.t[:], in_=vecs_d), reads=[const_b], writes=[vecs.b])
            emit_casts(12)

            def mem_attention(l):
                def scores(hm):
                    pl = []
                    for mb in range(2):
                        sc = bank()
                        P.op("pe", mk("matmul", sc.t[:, :], lhsT=mkt[l].t[:, hm, mb * 128:(mb + 1) * 128],
                                      rhs=mqt.t[:, hm, :], start=True, stop=True),
                             reads=[mkt[l].b, mqt.bufs[hm]], writes=[sc.b])
                        pt = ptb()
                        P.op("act", mk("activation", out=pt.t[:], in_=sc.t[:, :], func=AF.Exp, scale=128.0 ** -0.5),
                             reads=[sc.b], writes=[pt.b])
                        pl.append(pt)
                    return pl

                def finish(hm, pl):
                    num = bank()
                    mm_group(num.t[:, :], num, [(mvt[l].t[:, mb, hm * 128:(hm + 1) * 128], pl[mb].t[:]) for mb in range(2)],
                             [mvt[l].b, pl[0].b, pl[1].b])
                    den = bank()
                    mm_group(den.t[:, :], den, [(onesb.t[:], pl[mb].t[:]) for mb in range(2)],
                             [onesb.b, pl[0].b, pl[1].b])
                    r = slot()
                    P.op("dve", mk("reciprocal", out=r.t[:, 0:NT], in_=den.t[:, :]), reads=[den.b], writes=[r.b])
                    P.op("dve", mk("tensor_tensor", out=mot.t[:, hm, :], in0=num.t[:, :], in1=r.t[:, 0:NT], op=ALU.mult),
                         reads=[num.b, r.b], writes=[mot.bufs[hm]])
                prev = None
                for hm in range(4):
                    pl = scores(hm)
                    if prev is not None:
                        finish(*prev)
                    prev = (hm, pl)
                finish(*prev)

            def out_proj(l, res=None):
                srcs = [(ot.t[:, c, :], ot.bufs[c]) for c in range(KC)] + [(mot.t[:, hm, :], mot.bufs[hm]) for hm in range(4)]
                for s in range(4):
                    w = W.get(("wo", l, s))
                    w3 = w.t[:, 0:12 * 256].rearrange("p (k n) -> p k n", n=256)
                    for oo in range(2):
                        oc = 2 * s + oo
                        bk = bank()
                        mm_group(bk.t[:, :], bk, [(w3[:, kc, oo * 128:(oo + 1) * 128], srcs[kc][0]) for kc in range(12)],
                                 [w.b] + [x[1] for x in srcs])
                        if res is None:
                            P.op("dve", mk("tensor_tensor", out=xt.t[:, oc, :], in0=bk.t[:, :], in1=xt.t[:, oc, :], op=ALU.add),
                                 reads=[bk.b], writes=[xt.bufs[oc]])
                        else:
                            P.op("dve", mk("tensor_tensor", out=xt.t[:, oc, :], in0=bk.t[:, :], in1=res[0][:, oc, :], op=ALU.add),
                                 reads=[bk.b] + list(res[1]), writes=[xt.bufs[oc]])

            def mlp(l, hb, hooks=()):
                hk = dict(hooks) if isinstance(hooks, dict) else {k: [f] for k, f in enumerate(hooks)}
                hc = [0]

                def H():
                    for f in hk.pop(hc[0], []):
                        f()
                    hc[0] += 1
                norm_part1(xt.t[:], xt.bufs, NT)
                H()
                rs = norm_part2(NT)
                apply_norm(hb, xt.t[:], xt.bufs, rs, NT, V_MLPG0 + l)
                H()
                for s in range(8):
                    w = W.get(("wu", l, s))
                    w3 = w.t[:].rearrange("p (k n) -> p k n", n=512)
                    for cc in range(4):
                        oc = 4 * s + cc
                        bk = bank()
                        mm_group(bk.t[:, :], bk, [(w3[:, kc, cc * 128:(cc + 1) * 128], hb.t[:, kc, :]) for kc in range(KC)],
                                 [w.b, hb.b])
                        r = slot()
                        P.op("act", mk("activation", out=r.t[:, 0:NT], in_=bk.t[:, :], func=AF.Relu), reads=[bk.b], writes=[r.b])
                        P.op("pool", mk("tensor_tensor", out=hid.t[:, oc, :], in0=r.t[:, 0:NT], in1=r.t[:, 0:NT], op=ALU.mult),
                             reads=[r.b], writes=[hid.bufs[oc]])
                    H()
                for oc in range(8):
                    w = W.get(("wd", l, oc))
                    w3 = w.t[:].rearrange("p (k n) -> p k n", n=128)
                    bk = bank()
                    mm_group(bk.t[:, :], bk, [(w3[:, kc, :], hid.t[:, kc, :]) for kc in range(32)], [w.b] + hid.bufs)
                    P.op("dve", mk("tensor_tensor", out=xt.t[:, oc, :], in0=bk.t[:, :], in1=xt.t[:, oc, :], op=ALU.add),
                         reads=[bk.b], writes=[xt.bufs[oc]])
                    H()
                for k in sorted(hk):
                    for f in hk[k]:
                        f()

            with contextlib.ExitStack() as st0:
                def tb0(name, shape, dt, nb=1):
                    return TB(st0.enter_context(nc.sbuf_tensor("S_" + name + tag, shape, dt)),
                              [Buf("%s.%d" % (name, i)) for i in range(nb)])

                slots0 = [tb0("slot%d" % i, [128, NT], F32) for i in range(8)]
                pts0 = [tb0("pt%d" % i, [128, NT], BF16) for i in range(12)]
                cur["slot"] = rr(slots0)
                cur["pt"] = rr(pts0)
                kth = tb0("kth", [128, 4, 6 * 128], BF16)
                vah = tb0("vah", [128, 6, 4, 128], BF16)
                ktile = tb0("ktile", [128, 4, NT], BF16)
                vtile = tb0("vtile", [128, 4, 4, 128], BF16)
                posi = tb0("posi", [128, NT], I32)
                ctab = tb0("ctab", [128, NT], F32)
                stab = tb0("stab", [128, NT], F32)
                qt = tb0("qt", [128, KC, NT], BF16, KC)
                cv = tb0("cv", [128, 2], F32)
                tri = tb0("tri", [128, 2, 128], BF16)
                sink2 = tb0("sink2", [2, 16], F32)
                es2 = tb0("es2", [2, 16], F32)
                hi2 = tb0("hi2", [2, 16], BF16)
                hif = tb0("hif", [2, 16], F32)
                cm = tb0("cm", [2, 2], F32)
                esr = tb0("esr", [128, 16, 128], BF16)
                selE = tb0("selE", [128, 128], BF16)
                selO = tb0("selO", [128, 128], BF16)

                P.dma(mk("dma_start", out=cv.t[:], in_=cv_d), reads=[const_b], writes=[cv.b])
                P.dma(mk("dma_start", out=tri.t[:], in_=tri_d), reads=[const_b], writes=[tri.b], eng="pool")
                P.dma(mk("dma_start", out=cm.t[:], in_=cm_d), reads=[const_b], writes=[cm.b])
                P.dma(mk("dma_start", out=sink2.t[:], in_=sinks_d.partition_broadcast(2)), reads=[const_b], writes=[sink2.b])
                P.op("act", mk("activation", out=es2.t[:], in_=sink2.t[:], func=AF.Exp), reads=[sink2.b], writes=[es2.b])
                P.op("dve", mk("tensor_copy", out=hi2.t[:], in_=es2.t[:]), reads=[es2.b], writes=[hi2.b])
                P.op("dve", mk("tensor_copy", out=hif.t[:], in_=hi2.t[:]), reads=[hi2.b], writes=[hif.b])
                P.op("dve", mk("tensor_tensor", out=es2.t[:], in0=es2.t[:], in1=hif.t[:], op=ALU.subtract),
                     reads=[es2.b, hif.b], writes=[es2.b])
                P.op("dve", mk("tensor_scalar", out=hif.t[:], in0=hif.t[:], scalar1=cm.t[:, 0:1], scalar2=None, op0=ALU.mult),
                     reads=[hif.b, cm.b], writes=[hif.b])
                P.op("dve", mk("scalar_tensor_tensor", out=hif.t[:], in0=es2.t[:], scalar=cm.t[:, 1:2], in1=hif.t[:],
                               op0=ALU.mult, op1=ALU.add), reads=[es2.b, hif.b, cm.b], writes=[hif.b])
                P.op("dve", mk("memset", esr.t[:], 0.0), writes=[esr.b])
                P.op("dve", mk("tensor_copy", out=esr.t[0:2, :, :], in_=bc_last(hif.t[:], 128)), reads=[hif.b], writes=[esr.b])
                P.op("dve", mk("memset", ktile.t[:], 0.0), writes=[ktile.b])
                P.op("dve", mk("memset", selE.t[:], 0.0), writes=[selE.b])
                P.op("dve", mk("memset", selE.t[0:2, 64:128], 1.0), writes=[selE.b])
                P.op("dve", mk("memset", selO.t[:], 0.0), writes=[selO.b])
                P.op("dve", mk("memset", selO.t[0:2, 0:64], 1.0), writes=[selO.b])
                P.op("dve", mk("memset", vtile.t[:], 1.0), writes=[vtile.b])

                memx = xt.t[:, :, 0:MEM]
                P.dma(mk("dma_start", out=memx, in_=memT3), reads=[const_b], writes=xt.bufs)
                rs = norm_stats(memx, xt.bufs, MEM)
                memh = hb2[0]
                apply_norm(memh, memx, xt.bufs, rs, MEM, V_MEMG)
                for l in range(2):
                    w = W.get(("mkv", l, 0))
                    w3 = w.t[:].rearrange("p (k n) -> p k n", n=512)
                    for hm in range(4):
                        bk = bank()
                        mm_group(bk.t[:, 0:MEM], bk, [(w3[:, kc, hm * 128:(hm + 1) * 128], memh.t[:, kc, 0:MEM]) for kc in range(KC)],
                                 [w.b, memh.b])
                        P.op("act", mk("copy", out=mkt[l].t[:, hm, :], in_=bk.t[:, 0:MEM]), reads=[bk.b], writes=[mkt[l].b])
                    w = W.get(("mkv", l, 1))
                    w3 = w.t[:].rearrange("p (k n) -> p k n", n=512)
                    for mb in range(2):
                        bk = bank()
                        mm_group(bk.t[:, :], bk, [(memh.t[:, kc, mb * 128:(mb + 1) * 128], w3[:, kc, :]) for kc in range(KC)],
                                 [w.b, memh.b])
                        P.op("act", mk("copy", out=mvt[l].t[:, mb, :], in_=bk.t[:, :]), reads=[bk.b], writes=[mvt[l].b])

                swapmask = list(range(32))
                for i in range(8):
                    swapmask[i], swapmask[8 + i] = 8 + i, i

                def rope_tables(t0):
                    P.dma(mk("dma_start", out=posi.t[:], in_=pos[0:1, t0:t0 + NT].partition_broadcast(128)),
                          reads=[const_b], writes=[posi.b])
                    pf = slot()
                    P.op("dve", mk("tensor_copy", out=pf.t[:, 0:NT], in_=posi.t[:]), reads=[posi.b], writes=[pf.b])
                    for col, offv, tab in ((0, 0.75, ctab), (1, 0.5, stab)):
                        u = slot()
                        P.op("dve", mk("tensor_scalar", out=u.t[:, 0:NT], in0=pf.t[:, 0:NT], scalar1=cv.t[:, col:col + 1],
                                       scalar2=offv, op0=ALU.mult, op1=ALU.add), reads=[pf.b, cv.b], writes=[u.b])
                        kiv = posi.t[:]
                        P.op("dve", mk("tensor_copy", out=kiv, in_=u.t[:, 0:NT]), reads=[u.b], writes=[posi.b])
                        kf = slot()
                        P.op("dve", mk("tensor_copy", out=kf.t[:, 0:NT], in_=kiv), reads=[posi.b], writes=[kf.b])
                        P.op("dve", mk("tensor_tensor", out=u.t[:, 0:NT], in0=u.t[:, 0:NT], in1=kf.t[:, 0:NT], op=ALU.subtract),
                             reads=[u.b, kf.b], writes=[u.b])
                        P.op("dve", mk("scalar_tensor_tensor", out=kf.t[:, 0:NT], in0=u.t[:, 0:NT], scalar=0.0, in1=u.t[:, 0:NT],
                                       op0=ALU.is_lt, op1=ALU.add), reads=[u.b], writes=[kf.b])
                        P.op("act", mk("activation", out=tab.t[:], in_=kf.t[:, 0:NT], func=AF.Sin,
                                       scale=2.0 * math.pi, bias=-math.pi), reads=[kf.b], writes=[tab.b])

                def rope(bk, out_ap, out_buf, add_eng, split=None):
                    sw = slot()
                    P.op("dve", mk("stream_shuffle", out=sw.t[:, 0:NT], in_=bk.t[:, :], mask=swapmask),
                         reads=[bk.b], writes=[sw.b])
                    P.op("dve", mk("tensor_tensor", out=sw.t[:, 0:NT], in0=sw.t[:, 0:NT], in1=stab.t[:], op=ALU.mult),
                         reads=[sw.b, stab.b], writes=[sw.b])
                    t1 = slot()
                    P.op("dve", mk("tensor_tensor", out=t1.t[:, 0:NT], in0=bk.t[:, :], in1=ctab.t[:], op=ALU.mult),
                         reads=[bk.b, ctab.b], writes=[t1.b])
                    if split is None:
                        P.op(add_eng, mk("tensor_tensor", out=out_ap, in0=t1.t[:, 0:NT], in1=sw.t[:, 0:NT], op=ALU.add),
                             reads=[t1.b, sw.b], writes=[out_buf])
                    else:
                        for (r0_, r1_, o_ap) in split:
                            P.op(add_eng, mk("tensor_tensor", out=o_ap, in0=t1.t[r0_:r1_, 0:NT], in1=sw.t[r0_:r1_, 0:NT], op=ALU.add),
                                 reads=[t1.b, sw.b], writes=[out_buf])

                xA = [(xt.t[:], xt.bufs), (xalt_ap, xalt.bufs)]

                def phA_load(i):
                    t0 = i * NT
                    x_ap, x_bufs = xA[i % 2]
                    P.dma(mk("dma_start", out=x_ap, in_=xT3[:, :, t0:t0 + NT]), reads=[xT_b], writes=x_bufs)

                def phA_front1(i):
                    x_ap, x_bufs = xA[i % 2]
                    norm_part1(x_ap, x_bufs, NT)

                def phA_front(i):
                    t0 = i * NT
                    x_ap, x_bufs = xA[i % 2]
                    hb = hb2[i % 2]
                    rs = norm_part2(NT)
                    apply_norm(hb, x_ap, x_bufs, rs, NT, V_MIXG0)
                    P.dma(mk("dma_start", out=hs[:, :, t0:t0 + NT], in_=hb.t[:]), reads=[hb.b], writes=[hs_b[i]])

                wkv = W.get(("kv0", 0))

                def phA_back(i):
                    t0 = i * NT
                    hb = hb2[i % 2]
                    w = wkv
                    w3 = w.t[:].rearrange("p (k n) -> p k n", n=512)
                    for p in range(2):
                        bk = bank()
                        mm_group(bk.t[:, :], bk, [(w3[:, kc, p * 128:(p + 1) * 128], hb.t[:, kc, :]) for kc in range(KC)],
                                 [w.b, hb.b])
                        rope(bk, None, ktile.b, "dve", split=[(0, 64, ktile.t[0:64, 2 * p, :]), (64, 128, ktile.t[64:128, 2 * p + 1, :])])
                    P.dma(mk("dma_start", out=kts[:, :, t0:t0 + NT], in_=ktile.t[:]), reads=[ktile.b], writes=[kts_b[i]])
                    for b in range(4):
                        bk = bank()
                        mm_group(bk.t[:, 0:256], bk, [(hb.t[:, kc, b * 128:(b + 1) * 128], w3[:, kc, 256:512]) for kc in range(KC)],
                                 [w.b, hb.b])
                        bk3 = bk.t[:, 0:256].rearrange("p (h d) -> p h d", d=64)
                        P.op("act", mk("copy", out=vtile.t[:, b, 0::2, 0:64], in_=bk3[:, 0::2, :]), reads=[bk.b], writes=[vtile.b])
                        P.op("act", mk("copy", out=vtile.t[:, b, 1::2, 64:128], in_=bk3[:, 1::2, :]), reads=[bk.b], writes=[vtile.b])
                    P.dma(mk("dma_start", out=vas[:, 4 * i:4 * i + 4, :], in_=vtile.t[:].rearrange("p b h d -> p b (h d)")),
                          reads=[vtile.b], writes=[vas_b[i]])

                phA_load(0)
                phA_load(1)
                phA_front1(0)
                phA_front(0)
                rope_tables(0)
                for i in range(NTILE):
                    if i + 2 < NTILE:
                        phA_load(i + 2)
                    if i + 1 < NTILE:
                        phA_front1(i + 1)
                    phA_back(i)
                    if i + 1 < NTILE:
                        phA_front(i + 1)
                    if i + 1 < NTILE:
                        rope_tables((i + 1) * NT)
                    emit_casts(1)

                def attention_pieces(i):
                    b0 = 4 * i
                    groups = [(b, p) for b in range(4) for p in range(2)]

                    def scores(b, p):
                        res = []
                        for e in range(2):
                            r0, r1 = e * 64, (e + 1) * 64
                            js = [j for j in (-1, 0, 1) if 0 <= b0 + b + j < NBLK]
                            lst = []
                            for j in js:
                                hbk = (b0 + b + j) - (b0 - 1)
                                sc = bank()
                                P.op("pe", mk("matmul", sc.t[:, :], lhsT=kth.t[:, 2 * p + e, hbk * 128:(hbk + 1) * 128],
                                              rhs=qt.t[:, 4 * p:4 * p + 4, b * 128:(b + 1) * 128], start=True, stop=True),
                                     reads=[kth.b] + qt.bufs[4 * p:4 * p + 4], writes=[sc.b])
                                pt = ptb()
                                P.op("act", mk("activation", out=pt.t[:], in_=sc.t[:, :], func=AF.Exp, scale=0.125),
                                     reads=[sc.b], writes=[pt.b])
                                if j != 0:
                                    jj = 0 if j < 0 else 1
                                    pt3 = pt.t[:].rearrange("p (g q) -> p g q", q=128)
                                    P.op("pool", mk("tensor_tensor", out=pt3, in0=pt3, in1=bc_mid(tri.t[:, jj, :], 4), op=ALU.mult),
                                         reads=[pt.b, tri.b], writes=[pt.b])
                                lst.append((hbk, pt))
                            res.append(lst)
                        return res

                    def finish(b, p, res):
                        pvs = []
                        for e in range(2):
                            hk = 2 * p + e
                            lst = res[e]
                            pv = bank()
                            sel = selE if e == 0 else selO
                            P.op("pe", mk("matmul", pv.t[:, :], lhsT=sel.t[:, :], rhs=esr.t[:, 4 * hk:4 * hk + 4, :], start=True, stop=False),
                                 reads=[sel.b, esr.b], writes=[pv.b], inc=False)
                            for idx, (hbk, pt) in enumerate(lst):
                                P.op("pe", mk("matmul", pv.t[:, :], lhsT=vah.t[:, hbk, hk, :], rhs=pt.t[:],
                                              start=False, stop=(idx == len(lst) - 1)),
                                     reads=[vah.b, pt.b], writes=[pv.b], inc=(idx == len(lst) - 1))
                            pvs.append(pv)
                        dd = slot()
                        P.op("act", mk("copy", out=dd.t[0:64, 0:NT], in_=pvs[0].t[64:128, :]), reads=[pvs[0].b], writes=[dd.b])
                        P.op("act", mk("copy", out=dd.t[64:128, 0:NT], in_=pvs[1].t[0:64, :]), reads=[pvs[1].b], writes=[dd.b])
                        P.op("dve", mk("reciprocal", out=dd.t[:, 0:NT], in_=dd.t[:, 0:NT]), reads=[dd.b], writes=[dd.b])
                        for e in range(2):
                            r0, r1 = e * 64, (e + 1) * 64
                            pvn = pvs[e].t[r0:r1, :].rearrange("p (g q) -> p g q", q=128)
                            dd3 = dd.t[r0:r1, 0:NT].rearrange("p (g q) -> p g q", q=128)
                            P.op("dve", mk("tensor_tensor", out=ot.t[r0:r1, 4 * p:4 * p + 4, b * 128:(b + 1) * 128], in0=pvn, in1=dd3,
                                           op=ALU.mult), reads=[pvs[e].b, dd.b], writes=ot.bufs[4 * p:4 * p + 4])
                    stt = {"prev": None}

                    def piece(g):
                        def f():
                            if g is not None:
                                res = scores(*g)
                            if stt["prev"] is not None:
                                finish(*stt["prev"])
                            stt["prev"] = (g[0], g[1], res) if g is not None else None
                        return f
                    return [piece(g) for g in groups] + [piece(None)]

                def frontA(i):
                    t0 = i * NT
                    b0 = 4 * i
                    hb = hb2[i % 2]
                    P.dma(mk("dma_start", out=hb.t[:], in_=hs[:, :, t0:t0 + NT]), reads=[hs_b[i]], writes=[hb.b])
                    lo = max(0, b0 - 1)
                    hi = min(NBLK, b0 + 5)
                    h0 = lo - (b0 - 1)
                    nb_ = hi - lo
                    tl = sorted(set([(lo * 128) // NT, i, ((hi - 1) * 128) // NT]))
                    P.dma(mk("dma_start", out=kth.t[:, :, h0 * 128:(h0 + nb_) * 128], in_=kts[:, :, lo * 128:hi * 128]),
                          reads=[kts_b[j] for j in tl], writes=[kth.b])
                    P.dma(mk("dma_start", out=vah.t[:, h0:h0 + nb_, :, :].rearrange("p b h d -> p b (h d)"), in_=vas[:, lo:hi, :]),
                          reads=[vas_b[j] for j in tl], writes=[vah.b])
                    rope_tables(t0)
                    qslice(i, 0)

                def qslice(i, s):
                    hb = hb2[i % 2]
                    w = W.get(("q0", s))
                    w3 = w.t[:].rearrange("p (k n) -> p k n", n=512)
                    for cc in range(4):
                        c = 4 * s + cc
                        bk = bank()
                        mm_group(bk.t[:, :], bk, [(w3[:, kc, cc * 128:(cc + 1) * 128], hb.t[:, kc, :]) for kc in range(KC)],
                                 [w.b, hb.b])
                        rope(bk, qt.t[:, c, :], qt.bufs[c], "pool")

                def frontB(i):
                    hb = hb2[i % 2]
                    qslice(i, 1)
                    w = W.get(("mq0", 0))
                    w3 = w.t[:].rearrange("p (k n) -> p k n", n=512)
                    for hm in range(4):
                        bk = bank()
                        mm_group(bk.t[:, :], bk, [(w3[:, kc, hm * 128:(hm + 1) * 128], hb.t[:, kc, :]) for kc in range(KC)],
                                 [w.b, hb.b])
                        P.op("act", mk("copy", out=mqt.t[:, hm, :], in_=bk.t[:, :]), reads=[bk.b], writes=[mqt.bufs[hm]])

                def xt_load(i):
                    t0 = i * NT
                    P.dma(mk("dma_start", out=xt.t[:], in_=xT3[:, :, t0:t0 + NT]), reads=[xT_b], writes=xt.bufs)

                def l1_finish(i):
                    t0 = i * NT
                    hbl = hb2[i % 2]
                    rs = norm_part2(NT)
                    apply_norm(hbl, xt.t[:], xt.bufs, rs, NT, V_MIXG1)
                    P.dma(mk("dma_start", out=hs1[:, :, t0:t0 + NT], in_=hbl.t[:]), reads=[hbl.b], writes=[hs1_b[i]])

                def l1_xb(i):
                    t0 = i * NT
                    hbl = hb2[i % 2]
                    for s in range(2):
                        w = W.get(("xb1", s))
                        w3 = w.t[:].rearrange("p (k n) -> p k n", n=512)
                        for cc in range(4):
                            c = 4 * s + cc
                            bk = bank()
                            mm_group(bk.t[:, :], bk, [(w3[:, kc, cc * 128:(cc + 1) * 128], hbl.t[:, kc, :]) for kc in range(KC)],
                                     [w.b, hbl.b])
                            P.op("act", mk("copy", out=xalt_ap[:, c, :], in_=bk.t[:, :]), reads=[bk.b], writes=xalt.bufs)
                    P.dma(mk("dma_start", out=xbs[:, :, 2 + t0:2 + t0 + NT], in_=xalt_ap), reads=xalt.bufs, writes=[xbs_b[i]])

                frontA(0)
                frontB(0)
                for f in attention_pieces(0):
                    f()
                xt_load(0)
                mem_attention(0)
                for i in range(NTILE):
                    t0 = i * NT
                    hb = hb2[i % 2]
                    out_proj(0)
                    hooks = []
                    if i + 1 < NTILE:
                        ap_ = attention_pieces(i + 1)
                        hooks = {0: [(lambda i=i: frontA(i + 1))], 1: [(lambda i=i: frontB(i + 1))]}
                        for k_, idx_ in enumerate((3, 5, 7, 9, 10, 11, 12, 13, 14)):
                            hooks[idx_] = [ap_[k_]]
                        hooks[15] = [lambda: mem_attention(0)]
                    mlp(0, hb, hooks)
                    P.dma(mk("dma_start", out=x1s[:, :, t0:t0 + NT], in_=xt.t[:]), reads=xt.bufs, writes=[x1s_b[i]])
                    norm_part1(xt.t[:], xt.bufs, NT)
                    l1_finish(i)
                    if i + 1 < NTILE:
                        xt_load(i + 1)
                    l1_xb(i)
                    emit_casts(3)
            P.barrier()

            with contextlib.ExitStack() as st1:
                def tb1(name, shape, dt, nb=1):
                    return TB(st1.enter_context(nc.sbuf_tensor("S_" + name + tag, shape, dt)),
                              [Buf("%s.%d" % (name, i)) for i in range(nb)])

                lslots = [tb1("lslot%d" % i, [128, NT], F32) for i in range(16)]
                sslots = [tb1("sslot%d" % i, [128, NT], F32) for i in range(8)]
                xbhs = [tb1("xbh%d" % i, [128, NT + 4], F32) for i in range(4)]
                xbhn = rr(xbhs)
                pts1 = [tb1("ptl%d" % i, [128, NT], BF16) for i in range(4)]
                lslot = rr(lslots)
                cur["slot"] = rr(sslots)
                cur["pt"] = rr(pts1)
                wgate = tb1("wgate", [128, 4096], BF16)
                wg5 = wgate.t[:].rearrange("p (x d n e) -> p x d n e", x=2, d=2, n=8)
                dvec = tb1("dvec", [128, 6, 8], F32)
                dtmp = [tb1("dtmp%d" % i, [128, 2, 8], F32) for i in range(3)]
                carry = [tb1("carry%d" % i, [128, 8], F32) for i in range(2)]
                xcb3 = [tb1("xcb%d" % i, [128, NT], BF16) for i in range(4)]
                zpad = tb1("zpad", [128, 8, 2], F32)
                xcbn = rr(xcb3)

                for x_, src in enumerate((wa_d, wx_d)):
                    for d_ in range(2):
                        P.dma(mk("dma_start", out=wg5[:, x_, d_, :, :], in_=src[d_].rearrange("n k e -> k n e")),
                              reads=[const_b], writes=[wgate.b], eng="pool")
                lam = vecs.t[:, V_LAM:V_LAM + 2, :]
                e_, dn, z_ = dtmp
                P.op("act", mk("activation", out=e_.t[:], in_=lam, func=AF.Exp, scale=-1.0), reads=[vecs.b], writes=[e_.b])
                P.op("dve", mk("tensor_scalar", out=dn.t[:], in0=e_.t[:], scalar1=2.0, scalar2=None, op0=ALU.add),
                     reads=[e_.b], writes=[dn.b])
                P.op("dve", mk("reciprocal", out=dn.t[:], in_=dn.t[:]), reads=[dn.b], writes=[dn.b])
                P.op("dve", mk("tensor_tensor", out=z_.t[:], in0=e_.t[:], in1=dn.t[:], op=ALU.mult), reads=[e_.b, dn.b], writes=[z_.b])
                P.op("dve", mk("tensor_tensor", out=dn.t[:], in0=z_.t[:], in1=z_.t[:], op=ALU.mult), reads=[z_.b], writes=[dn.b])
                P.op("dve", mk("tensor_scalar", out=dn.t[:], in0=dn.t[:], scalar1=1.0 / 3.0, scalar2=1.0, op0=ALU.mult, op1=ALU.add),
                     reads=[dn.b], writes=[dn.b])
                P.op("dve", mk("tensor_tensor", out=dn.t[:], in0=dn.t[:], in1=z_.t[:], op=ALU.mult), reads=[dn.b, z_.b], writes=[dn.b])
                P.op("dve", mk("tensor_scalar", out=dvec.t[:, 0:2, :], in0=dn.t[:], scalar1=-8.0, scalar2=None, op0=ALU.mult),
                     reads=[dn.b], writes=[dvec.b])
                P.op("dve", mk("tensor_scalar", out=dvec.t[:, 2:4, :], in0=vecs.t[:, V_BA:V_BA + 2, :], scalar1=0.5, scalar2=None,
                               op0=ALU.mult), reads=[vecs.b], writes=[dvec.b])
                P.op("dve", mk("tensor_scalar", out=dvec.t[:, 4:6, :], in0=vecs.t[:, V_BX:V_BX + 2, :], scalar1=0.5, scalar2=None,
                               op0=ALU.mult), reads=[vecs.b], writes=[dvec.b])
                for cb_ in carry:
                    P.op("dve", mk("memset", cb_.t[:], 0.0), writes=[cb_.b])
                P.op("dve", mk("memset", zpad.t[:], 0.0), writes=[zpad.b])
                P.dma(mk("dma_start", out=xbs[:, :, 0:2], in_=zpad.t[:]), reads=[zpad.b], writes=[xbs_pad])
                P.dma(mk("dma_start", out=xbs[:, :, NTOK + 2:NTOK + 4], in_=zpad.t[:]), reads=[zpad.b], writes=[xbs_pad])

                def lru_s1a(d_, c, xc):
                    xcb = xcbn()
                    P.op("act", mk("copy", out=xcb.t[:], in_=xc.t[:, 0:NT]), reads=[xc.b], writes=[xcb.b])
                    pa = bank()
                    P.op("pe", mk("matmul", pa.t[:, :], lhsT=wg5[:, 0, d_, c, :], rhs=xcb.t[:], start=True, stop=True),
                         reads=[wgate.b, xcb.b], writes=[pa.b])
                    px = bank()
                    P.op("pe", mk("matmul", px.t[:, :], lhsT=wg5[:, 1, d_, c, :], rhs=xcb.t[:], start=True, stop=True),
                         reads=[wgate.b, xcb.b], writes=[px.b])
                    return (pa, px)

                def lru_s1b(d_, c, xc, pp):
                    pa, px = pp
                    ta = lslot()
                    P.op("act", mk("activation", out=ta.t[:, 0:NT], in_=pa.t[:, :], func=AF.Tanh, scale=0.5,
                                   bias=dvec.t[:, 2 + d_, c:c + 1]), reads=[pa.b, dvec.b], writes=[ta.b])
                    tx = lslot()
                    P.op("act", mk("activation", out=tx.t[:, 0:NT], in_=px.t[:, :], func=AF.Tanh, scale=0.5,
                                   bias=dvec.t[:, 4 + d_, c:c + 1]), reads=[px.b, dvec.b], writes=[tx.b])
                    P.op("act", mk("activation", out=ta.t[:, 0:NT], in_=ta.t[:, 0:NT], func=AF.Exp,
                                   scale=dvec.t[:, d_, c:c + 1], bias=dvec.t[:, d_, c:c + 1]), reads=[ta.b, dvec.b], writes=[ta.b])
                    sq = lslot()
                    P.op("pool", mk("tensor_tensor", out=sq.t[:, 0:NT], in0=ta.t[:, 0:NT], in1=ta.t[:, 0:NT], op=ALU.mult),
                         reads=[ta.b], writes=[sq.b])
                    P.op("act", mk("activation", out=sq.t[:, 0:NT], in_=sq.t[:, 0:NT], func=AF.Relu, scale=-1.0, bias=CLAMP_C),
                         reads=[sq.b], writes=[sq.b])
                    P.op("dve", mk("scalar_tensor_tensor", out=tx.t[:, 0:NT], in0=tx.t[:, 0:NT], scalar=1.0, in1=xc.t[:, 0:NT],
                                   op0=ALU.add, op1=ALU.mult), reads=[tx.b, xc.b], writes=[tx.b])
                    return (ta, tx, sq)

                def lru_sqrt(st_):
                    sq = st_[2]
                    P.op("act", mk("activation", out=sq.t[:, 0:NT], in_=sq.t[:, 0:NT], func=AF.Sqrt, scale=1.0, bias=1.0 - CLAMP_C),
                         reads=[sq.b], writes=[sq.b])

                def lru_s2(c, st_, reverse, cb_, do_sqrt=True):
                    ta, tx, sq = st_
                    if do_sqrt:
                        lru_sqrt(st_)
                    P.op("dve", mk("scalar_tensor_tensor", out=tx.t[:, 0:NT], in0=tx.t[:, 0:NT], scalar=0.5, in1=sq.t[:, 0:NT],
                                   op0=ALU.mult, op1=ALU.mult), reads=[tx.b, sq.b], writes=[tx.b])
                    hh = slot()
                    if not reverse:
                        P.op("dve", mk("tensor_tensor_scan", out=hh.t[:, 0:NT], data0=ta.t[:, 0:NT], data1=tx.t[:, 0:NT],
                                       initial=cb_.t[:, c:c + 1], op0=ALU.mult, op1=ALU.add),
                             reads=[ta.b, tx.b, cb_.b], writes=[hh.b])
                        P.op("dve", mk("tensor_copy", out=cb_.t[:, c:c + 1], in_=hh.t[:, NT - 1:NT]), reads=[hh.b], writes=[cb_.b])
                    else:
                        P.op("dve", mk("tensor_tensor_scan", out=hh.t[:, 0:NT][:, ::-1], data0=ta.t[:, 0:NT][:, ::-1],
                                       data1=tx.t[:, 0:NT][:, ::-1], initial=cb_.t[:, c:c + 1], op0=ALU.mult, op1=ALU.add),
                             reads=[ta.b, tx.b, cb_.b], writes=[hh.b])
                        P.op("dve", mk("tensor_copy", out=cb_.t[:, c:c + 1], in_=hh.t[:, 0:1]), reads=[hh.b], writes=[cb_.b])
                    return hh

                def p1_load(i, c):
                    t0 = i * NT
                    tl = sorted(set([max(0, i - 1), i, min(NTILE - 1, i + 1)]))
                    xbh = xbhn()
                    P.dma(mk("dma_start", out=xbh.t[:, 0:NT + 4], in_=xbs[:, c, t0:t0 + NT + 4]),
                          reads=[xbs_b[j] for j in tl] + [xbs_pad], writes=[xbh.b])
                    return xbh

                def p1_s1a(i, c, xbh):
                    t0 = i * NT
                    xc = slot()
                    P.op("pool", mk("tensor_scalar", out=xc.t[:, 0:NT], in0=xbh.t[:, 1:1 + NT],
                                    scalar1=vecs.t[:, V_CONVW, c:c + 1], scalar2=vecs.t[:, V_CONVB, c:c + 1],
                                    op0=ALU.mult, op1=ALU.add), reads=[xbh.b, vecs.b], writes=[xc.b])
                    for tap in range(1, 4):
                        P.op("dve", mk("scalar_tensor_tensor", out=xc.t[:, 0:NT], in0=xbh.t[:, 1 + tap:1 + tap + NT],
                                       scalar=vecs.t[:, V_CONVW + tap, c:c + 1], in1=xc.t[:, 0:NT],
                                       op0=ALU.mult, op1=ALU.add), reads=[xbh.b, xc.b, vecs.b], writes=[xc.b])
                    P.dma(mk("dma_start", out=xcs[:, c, t0:t0 + NT], in_=xc.t[:, 0:NT]), reads=[xc.b], writes=[xcs_b[i]])
                    return (xc, lru_s1a(0, c, xc))

                def p1_s2(i, c, st_):
                    t0 = i * NT
                    hh = lru_s2(c, st_, False, carry[0], do_sqrt=False)
                    P.dma(mk("dma_start", out=h1s[:, c, t0:t0 + NT], in_=hh.t[:, 0:NT]), reads=[hh.b], writes=[h1s_b[i]])

                seq1 = [(i, c) for i in range(NTILE) for c in range(KC)]
                pairs1 = [seq1[n_:n_ + 2] for n_ in range(0, len(seq1), 2)]
                loaded = [p1_load(i, c) for (i, c) in pairs1[0]]
                pend = []
                for j, pr in enumerate(pairs1):
                    nxt = [p1_load(i, c) for (i, c) in pairs1[j + 1]] if j + 1 < len(pairs1) else []
                    for (i, c, st_) in pend:
                        lru_sqrt(st_)
                    sa = [(i, c, p1_s1a(i, c, xb_)) for (i, c), xb_ in zip(pr, loaded)]
                    for (i, c, st_) in pend:
                        p1_s2(i, c, st_)
                    pend = [(i, c, lru_s1b(0, c, xc, pp)) for (i, c, (xc, pp)) in sa]
                    loaded = nxt
                for (i, c, st_) in pend:
                    lru_sqrt(st_)
                for (i, c, st_) in pend:
                    p1_s2(i, c, st_)

                def p2_load(i, c):
                    t0 = i * NT
                    xc = xbhn()
                    P.dma(mk("dma_start", out=xc.t[:, 0:NT], in_=xcs[:, c, t0:t0 + NT]), reads=[xcs_b[i]], writes=[xc.b])
                    return xc

                def p2_s1a(i, c, xc):
                    t0 = i * NT
                    h1 = lslot()
                    P.dma(mk("dma_start", out=h1.t[:, 0:NT], in_=h1s[:, c, t0:t0 + NT]), reads=[h1s_b[i]], writes=[h1.b])
                    return (xc, h1, lru_s1a(1, c, xc))

                def p2_s2(i, c, st_, hb, wg_):
                    hh = lru_s2(c, st_[0:3], True, carry[1])
                    h1 = st_[3]
                    wgt, wgt3 = wg_
                    cc = c % 4
                    bk = bank()
                    mm_group(bk.t[:, :], bk, [(wgt3[:, kc, cc * 128:(cc + 1) * 128], hb.t[:, kc, :]) for kc in range(KC)],
                             [wgt.b, hb.b])
                    gg = slot()
                    P.op("act", mk("activation", out=gg.t[:, 0:NT], in_=bk.t[:, :], func=AF.Gelu_apprx_tanh),
                         reads=[bk.b], writes=[gg.b])
                    P.op("pool", mk("tensor_tensor", out=h1.t[:, 0:NT], in0=h1.t[:, 0:NT], in1=hh.t[:, 0:NT], op=ALU.add),
                         reads=[h1.b, hh.b], writes=[h1.b])
                    P.op("pool", mk("tensor_tensor", out=ot.t[:, c, :], in0=h1.t[:, 0:NT], in1=gg.t[:, 0:NT], op=ALU.mult),
                         reads=[h1.b, gg.b], writes=[ot.bufs[c]])

                def p2_prefetch(i):
                    t0 = i * NT
                    hb = hb2[i % 2]
                    P.dma(mk("dma_start", out=hb.t[:], in_=hs1[:, :, t0:t0 + NT]), reads=[hs1_b[i]], writes=[hb.b])
                    return [p2_load(i, c) for c in (0, 1)]

                order2 = list(range(NTILE - 1, -1, -1))

                def lru_pieces(i, first_ld):
                    hb = hb2[i % 2]
                    stt = {"loaded": first_ld, "pend": [], "wg": None}

                    def s1(n_):
                        def f():
                            nxt = [p2_load(i, c) for c in (n_ + 2, n_ + 3)] if n_ + 2 < KC else []
                            sa = [(c, p2_s1a(i, c, ld)) for c, ld in zip((n_, n_ + 1), stt["loaded"])]
                            stt["new"] = [(c, lru_s1b(1, c, xc, pp) + (h1,)) for (c, (xc, h1, pp)) in sa]
                            stt["loaded"] = nxt
                        return f

                    def s2():
                        def f():
                            for (c, st_) in stt["pend"]:
                                if c % 2 == 0:
                                    wgt = W.get(("gate1", c // 4))
                                    stt["wg"] = (wgt, wgt.t[:].rearrange("p (k n) -> p k n", n=512))
                                p2_s2(i, c, st_, hb, stt["wg"])
                            stt["pend"] = stt.pop("new", [])
                        return f
                    pcs = []
                    for n_ in range(0, KC, 2):
                        pcs.append(s1(n_))
                        pcs.append(s2())
                    pcs.append(s2())
                    return pcs

                def mq_proj(i):
                    hb = hb2[i % 2]
                    w = W.get(("mq1", 0))
                    w3 = w.t[:].rearrange("p (k n) -> p k n", n=512)
                    for hm in range(4):
                        bk = bank()
                        mm_group(bk.t[:, :], bk, [(w3[:, kc, hm * 128:(hm + 1) * 128], hb.t[:, kc, :]) for kc in range(KC)],
                                 [w.b, hb.b])
                        P.op("act", mk("copy", out=mqt.t[:, hm, :], in_=bk.t[:, :]), reads=[bk.b], writes=[mqt.bufs[hm]])

                def x1_load(i):
                    t0 = i * NT
                    P.dma(mk("dma_start", out=xalt_ap, in_=x1s[:, :, t0:t0 + NT]), reads=[x1s_b[i]], writes=xalt.bufs)

                first_ld = p2_prefetch(order2[0])
                for f in lru_pieces(order2[0], first_ld):
                    f()
                mq_proj(order2[0])
                x1_load(order2[0])
                mem_attention(1)
                for oi, i in enumerate(order2):
                    t0 = i * NT
                    hb = hb2[i % 2]
                    out_proj(1, res=(xalt_ap, xalt.bufs))
                    hooks = {}
                    nx = None
                    if oi + 1 < len(order2):
                        nx = order2[oi + 1]
                        fl = p2_prefetch(nx)
                        lp_ = lru_pieces(nx, fl)
                        hooks = {0: [(lambda nx=nx: mq_proj(nx))]}
                        for k_, idx_ in enumerate((2, 4, 6, 8, 10, 11, 12, 13, 14)):
                            hooks[idx_] = [lp_[k_]]
                        hooks[15] = [lambda: mem_attention(1)]
                    mlp(1, hb, hooks)
                    if nx is not None:
                        x1_load(nx)
                    norm_part1(xt.t[:], xt.bufs, NT)
                    rs = norm_part2(NT)
                    for c in range(KC):
                        P.op("dve", mk("scalar_tensor_tensor", out=sqb_ap[:, c, :], in0=xt.t[:, c, :],
                                       scalar=vecs.t[:, V_FING, c:c + 1], in1=rs.t[:, 0:NT], op0=ALU.mult, op1=ALU.mult),
                             reads=[xt.bufs[c], rs.b, vecs.b], writes=sqb.bufs)
                    P.dma(mk("dma_start", out=outT3[:, :, t0:t0 + NT], in_=sqb_ap), reads=sqb.bufs, writes=[out_b])
            P.barrier()

        Pd = Prog()
        Wd = WRing(Pd, ring, wscr, wscr_b, plan, None, None)
        record(Pd, Wd, "_d")
        seq = list(Wd.rec)
        for o in Buf.ALL:
            o.w = None
            o.r = []
        P = Prog()
        W = WRing(P, ring, wscr, wscr_b, plan, seq, None)
        record(P, W, "_r")
        print("recorded ops:", P.nops, {e: len(q) for e, q in P.q.items()}, "epochs", P.epoch, flush=True)
        P.emit(nc)
    return nc


_QPERM = None


def _qperm():
    cols = []
    for p in range(2):
        for g in range(4):
            for hd in (8 * p + g, 8 * p + 4 + g):
                cols.extend(range(hd * 64, (hd + 1) * 64))
    return np.array(cols, dtype=np.int64)


def _vec128(v):
    return np.ascontiguousarray(np.asarray(v, dtype=np.float32).reshape(8, 128).T)


def kernel(x, mem, positions, mix_norm, mlp_norm, mem_norm, final_norm, w_mem_kv, w_out,
           w_up, w_down, attn_w_in, attn_sinks, lru_w_in, lru_conv_w, lru_conv_b,
           lru_wa, lru_ba, lru_wx, lru_bx, lru_lambda):
    x = np.asarray(x, dtype=np.float32)
    mem = np.asarray(mem, dtype=np.float32)
    positions = np.asarray(positions).astype(np.int32)
    qp = _qperm()
    w_in0 = np.asarray(attn_w_in, dtype=np.float32)[0]
    w_in0 = np.ascontiguousarray(np.concatenate([w_in0[:, qp], w_in0[:, 1024:]], axis=1))
    w_out_p = np.array(w_out, dtype=np.float32, copy=True)
    w_out_p[0, :1024] = w_out_p[0, :1024][qp]
    vec_list = [mix_norm[0], mix_norm[1], mlp_norm[0], mlp_norm[1], mem_norm, final_norm, lru_conv_b[0],
                lru_conv_w[0][0], lru_conv_w[0][1], lru_conv_w[0][2], lru_conv_w[0][3],
                lru_ba[0][0], lru_ba[0][1], lru_bx[0][0], lru_bx[0][1], lru_lambda[0][0], lru_lambda[0][1]]
    vecs = np.ascontiguousarray(np.stack([_vec128(v) for v in vec_list], axis=1))
    invf = (np.float32(500000.0) ** (-2.0 * np.arange(8, dtype=np.float32) / np.float32(16.0))).astype(np.float32)
    cv = np.zeros((128, 2), np.float32)
    for p in range(128):
        d = p % 64
        if d < 16:
            f = float(invf[d % 8]) / (2.0 * math.pi)
            cv[p, 0] = f
            cv[p, 1] = -f if d < 8 else f
    kk = np.arange(128)[:, None]
    qq = np.arange(128)[None, :]
    tri = np.stack([(kk >= qq), (kk <= qq)], axis=1).astype(np.float32)
    shared = {
        "vecs": vecs, "cv": cv, "tri": np.ascontiguousarray(tri), "cm": np.eye(2, dtype=np.float32),
        "sinks": np.ascontiguousarray(np.asarray(attn_sinks, np.float32).reshape(1, 16)),
        "w_in0": w_in0, "w_in1": np.ascontiguousarray(np.asarray(lru_w_in, np.float32)[0]),
        "w_out": w_out_p, "w_up": np.asarray(w_up, np.float32), "w_down": np.asarray(w_down, np.float32),
        "w_mem_kv": np.asarray(w_mem_kv, np.float32),
        "lru_wa": np.ascontiguousarray(np.asarray(lru_wa, np.float32)[0]),
        "lru_wx": np.ascontiguousarray(np.asarray(lru_wx, np.float32)[0]),
    }
    real = {0: 0, 1: 1, 4: 2, 5: 3} if SPREAD8 else {0: 0, 1: 1, 2: 2, 3: 3}
    zero_shared = {k: np.zeros_like(v) for k, v in shared.items()}
    zx = np.zeros((D, NTOK), np.float32)
    zm = np.zeros((D, MEM), np.float32)
    zp = np.zeros((1, NTOK), np.int32)
    in_maps = []
    for c in range(8 if SPREAD8 else 4):
        if c in real:
            b = real[c]
            m = dict(shared)
            m["xT"] = np.ascontiguousarray(x[b].T)
            m["memT"] = np.ascontiguousarray(mem[b].T)
            m["pos"] = np.ascontiguousarray(positions[b][None, :])
        else:
            m = dict(zero_shared)
            m["xT"], m["memT"], m["pos"] = zx, zm, zp
        in_maps.append(m)
    nc = build_program()
    res = run_bass_kernel_spmd(nc, in_maps, core_ids=list(range(len(in_maps))))
    out = np.empty((4, NTOK, D), np.float32)
    for c, b in real.items():
        out[b] = res.results[c]["outT"].T
    return out
```

```python
import contextlib
import math
import numpy as np
import concourse.bass as bass
import concourse.mybir as mybir
from concourse.ap import AP
from concourse.bass_utils import run_bass_kernel_spmd

F32 = mybir.dt.float32
BF16 = mybir.dt.bfloat16
I32 = mybir.dt.int32
AF = mybir.ActivationFunctionType
ALU = mybir.AluOpType
AX = mybir.AxisListType

D = 1024
KC = 8
NT = 512
DFF = 4096
MEM = 256
EPS = 1e-6
NCORES = 4
NTOK = 8192
NBLK = NTOK // 128
NTILE = NTOK // NT
RING = 5
NSLOT = 12
SEM_LIMIT = 30000
SPREAD8 = True
CLAMP_C = float(np.float32(1.0) - np.float32(2.0 ** -24))

V_MIXG0, V_MIXG1, V_MLPG0, V_MLPG1, V_MEMG, V_FING, V_CONVB = 0, 1, 2, 3, 4, 5, 6
V_CONVW = 7
V_BA = 11
V_BX = 13
V_LAM = 15
NVEC = 17


def mk(name, *args, **kw):
    return lambda e: getattr(e, name)(*args, **kw)


def bc_mid(ap2, n):
    a = ap2.ap
    return AP(ap2.tensor, ap2.offset, [list(a[0]), [0, n], list(a[1])])


def bc_last(ap2, n):
    a = ap2.ap
    return AP(ap2.tensor, ap2.offset, [list(a[0]), list(a[1]), [0, n]])


class Buf:
    __slots__ = ("name", "w", "r")
    ALL = []

    def __init__(self, name):
        self.name = name
        self.w = None
        self.r = []
        Buf.ALL.append(self)


class Prog:
    CE = ("pe", "act", "dve", "pool")
    ENG = ("pe", "act", "dve", "pool", "sp")

    def __init__(self, n_dma_sems=30):
        self.q = {e: [] for e in self.ENG}
        self.cnt = {e: 0 for e in self.CE}
        self.epoch = {e: 0 for e in self.CE}
        self.seen = {e: {} for e in self.ENG}
        self.n_dma = n_dma_sems
        self.dma_val = [0] * n_dma_sems
        self.dma_rr = 0
        self.n_g = 0
        self.nops = 0

    def _next_ev(self, eng):
        if self.cnt[eng] >= SEM_LIMIT:
            return ((eng, self.epoch[eng] + 1), 1)
        return ((eng, self.epoch[eng]), self.cnt[eng] + 1)

    def _commit(self, eng):
        if self.cnt[eng] >= SEM_LIMIT:
            self.epoch[eng] += 1
            self.cnt[eng] = 0
        self.cnt[eng] += 1
        return ((eng, self.epoch[eng]), self.cnt[eng])

    def _deps(self, eng, reads, writes):
        need = {}

        def add(ev):
            if ev is None:
                return
            k, v = ev
            if need.get(k, 0) < v:
                need[k] = v
        for b in reads:
            add(b.w)
        for b in writes:
            add(b.w)
            for ev in b.r:
                add(ev)
        waits = []
        seen = self.seen[eng]
        for k, v in need.items():
            if eng == "pe" and k[0] == "pe":
                continue
            if seen.get(k, 0) >= v:
                continue
            seen[k] = v
            waits.append((k, v))
        return waits

    def _mark(self, ev, reads, writes):
        for b in reads:
            b.r.append(ev)
            if len(b.r) > 64:
                m = {}
                for k, v in b.r:
                    if m.get(k, 0) < v:
                        m[k] = v
                b.r = list(m.items())
        for b in writes:
            b.w = ev
            b.r = []

    def op(self, eng, fn, reads=(), writes=(), inc=True):
        waits = self._deps(eng, reads, writes)
        if inc:
            ev = self._commit(eng)
            self.q[eng].append((waits, fn, ev))
        else:
            ev = self._next_ev(eng)
            self.q[eng].append((waits, fn, None))
        self._mark(ev, reads, writes)
        self.nops += 1
        return ev

    def dma(self, fn, reads=(), writes=(), eng="sp"):
        if eng == "pool":
            key = ("g", self.n_g)
            self.n_g += 1
            waits = self._deps(eng, reads, writes)
            ev = (key, 16)
            self.q[eng].append((waits, fn, ev))
            self._mark(ev, reads, writes)
            self.nops += 1
            return ev
        s = self.dma_rr
        self.dma_rr = (self.dma_rr + 1) % self.n_dma
        key = ("d", s)
        waits = self._deps(eng, reads, writes)
        pv = self.dma_val[s]
        if pv > 0 and self.seen[eng].get(key, 0) < pv:
            self.seen[eng][key] = pv
            waits.append((key, pv))
        self.dma_val[s] += 16
        ev = (key, self.dma_val[s])
        self.q[eng].append((waits, fn, ev))
        self._mark(ev, reads, writes)
        self.nops += 1
        return ev

    def barrier(self):
        for e in self.ENG:
            waits = []
            for f in self.CE:
                if f == e or self.cnt[f] == 0:
                    continue
                k, v = (f, self.epoch[f]), self.cnt[f]
                if self.seen[e].get(k, 0) < v:
                    self.seen[e][k] = v
                    waits.append((k, v))
            for s in range(self.n_dma):
                k, v = ("d", s), self.dma_val[s]
                if v > 0 and self.seen[e].get(k, 0) < v:
                    self.seen[e][k] = v
                    waits.append((k, v))
            for s in range(self.n_g):
                k, v = ("g", s), 16
                if self.seen[e].get(k, 0) < v:
                    self.seen[e][k] = v
                    waits.append((k, v))
            if waits:
                self.q[e].append((waits, None, None))

    def emit(self, nc):
        with contextlib.ExitStack() as st:
            sems = {}
            for e in self.CE:
                for ep in range(self.epoch[e] + 1):
                    sems[(e, ep)] = st.enter_context(nc.semaphore("s_%s%d" % (e, ep)))
            for i in range(self.n_dma):
                sems[("d", i)] = st.enter_context(nc.semaphore("s_d%d" % i))
            for i in range(self.n_g):
                sems[("g", i)] = st.enter_context(nc.semaphore("s_g%d" % i))
            block = st.enter_context(nc.Block())
            prog = self

            def run(engname, engobj):
                for waits, fn, ev in prog.q[engname]:
                    for k, v in waits:
                        engobj.wait_ge(sems[k], v)
                    if fn is None:
                        continue
                    ins = fn(engobj)
                    if ev is not None:
                        k, v = ev
                        ins.then_inc(sems[k], 16 if k[0] in ("d", "g") else 1)

            @block.tensor
            def _(e):
                run("pe", e)

            @block.scalar
            def _(e):
                run("act", e)

            @block.vector
            def _(e):
                run("dve", e)

            @block.gpsimd
            def _(e):
                run("pool", e)

            @block.sync
            def _(e):
                run("sp", e)


class TB:
    def __init__(self, t, bufs):
        self.t = t
        self.bufs = bufs
        self.b = bufs[0]

    def __getitem__(self, k):
        return self.t[k]


def weight_plan():
    sl = {}
    off = [0]

    def add(key, src, k0, kcs, c0, ncs, lsel=None):
        sl[key] = dict(src=src, k0=k0, kcs=kcs, c0=c0, ncs=ncs, off=off[0], lsel=lsel)
        off[0] += kcs * ncs

    add(("kv0", 0), "w_in0", 0, 8, 1024, 512)
    for s in range(2):
        add(("q0", s), "w_in0", 0, 8, 512 * s, 512)
    add(("mq0", 0), "w_in0", 0, 8, 1536, 512)
    for s in range(2):
        add(("xb1", s), "w_in1", 0, 8, 512 * s, 512)
    for s in range(2):
        add(("gate1", s), "w_in1", 0, 8, 1024 + 512 * s, 512)
    add(("mq1", 0), "w_in1", 0, 8, 2048, 512)
    for l in range(2):
        for s in range(2):
            add(("mkv", l, s), "w_mem_kv", 0, 8, 512 * s, 512, lsel=l)
        for s in range(4):
            add(("wo", l, s), "w_out", 0, 12, 256 * s, 256, lsel=l)
        for s in range(8):
            add(("wu", l, s), "w_up", 0, 8, 512 * s, 512, lsel=l)
        for s in range(8):
            add(("wd", l, s), "w_down", 0, 32, 128 * s, 128, lsel=l)
    return sl, off[0]


class WRing:
    def __init__(self, P, slots, wscr, wscr_bufs, plan, seq, need_cast):
        self.P = P
        self.slots = slots
        self.wscr = wscr
        self.wb = wscr_bufs
        self.plan = plan
        self.seq = seq
        self.rec = []
        self.pos = 0
        self.issued = 0
        self.R = len(slots)
        self.need_cast = need_cast

    def get(self, key):
        k = self.pos
        self.pos += 1
        self.rec.append(key)
        if self.seq is None:
            return self.slots[k % self.R]
        assert self.seq[k] == key, (self.seq[k], key)
        upto = min(len(self.seq), k + self.R)
        while self.issued < upto:
            j = self.issued
            kk = self.seq[j]
            self.need_cast(kk)
            d = self.plan[kk]
            n = d["kcs"] * d["ncs"]
            slot = self.slots[j % self.R]
            self.P.dma(mk("dma_start", out=slot.t[:, 0:n], in_=self.wscr[:, d["off"]:d["off"] + n]),
                       reads=[self.wb[kk]], writes=[slot.b])
            self.issued += 1
        return self.slots[k % self.R]


def build_program():
    nc = bass.Bass("TRN2", target_bir_lowering=False)
    plan, wtot = weight_plan()

    def din(name, shape, dt=F32):
        return nc.dram_tensor(name, shape, dt, kind="ExternalInput").ap()

    xT = din("xT", [D, NTOK])
    memT = din("memT", [D, MEM])
    pos = din("pos", [1, NTOK], I32)
    vecs_d = din("vecs", [128, NVEC, 8])
    cv_d = din("cv", [128, 2])
    tri_d = din("tri", [128, 2, 128])
    sinks_d = din("sinks", [1, 16])
    cm_d = din("cm", [2, 2])
    wsrc = {
        "w_in0": din("w_in0", [D, 2048]),
        "w_in1": din("w_in1", [D, 2560]),
        "w_out": din("w_out", [2, 1536, D]),
        "w_up": din("w_up", [2, D, DFF]),
        "w_down": din("w_down", [2, DFF, D]),
        "w_mem_kv": din("w_mem_kv", [2, D, D]),
    }
    wa_d = din("lru_wa", [2, 8, 128, 128])
    wx_d = din("lru_wx", [2, 8, 128, 128])
    outT = nc.dram_tensor("outT", [D, NTOK], F32, kind="ExternalOutput").ap()

    def dscr(name, shape, dt):
        return nc.dram_tensor(name, shape, dt, kind="Internal").ap()

    wscr = dscr("wscr", [128, wtot], BF16)
    hs = dscr("hs", [128, KC, NTOK], BF16)
    hs1 = dscr("hs1", [128, KC, NTOK], BF16)
    kts = dscr("kts", [128, 4, NTOK], BF16)
    vas = dscr("vas", [128, NBLK, 512], BF16)
    x1s = dscr("x1s", [128, KC, NTOK], F32)
    xbs = dscr("xbs", [128, KC, NTOK + 4], F32)
    xcs = dscr("xcs", [128, KC, NTOK], F32)
    h1s = dscr("h1s", [128, KC, NTOK], F32)

    xT3 = xT.rearrange("(c p) t -> p c t", p=128)
    outT3 = outT.rearrange("(c p) t -> p c t", p=128)
    memT3 = memT.rearrange("(c p) t -> p c t", p=128)

    st = contextlib.ExitStack()
    with st:
        def sb(name, shape, dt):
            return st.enter_context(nc.sbuf_tensor("S_" + name, shape, dt))

        def tb(name, shape, dt, nb=1):
            return TB(sb(name, shape, dt), [Buf("%s.%d" % (name, i)) for i in range(nb)])

        xt = tb("xt", [128, KC, NT], F32, KC)
        hb2 = [tb("hb%d" % i, [128, KC, NT], BF16) for i in range(2)]
        hid = tb("hid", [128, 32, NT], BF16, 32)
        hidF = hid.t[:].rearrange("p a b -> p (a b)").bitcast(F32).rearrange("p (a b) -> p a b", b=NT)
        sqb = TB(None, hid.bufs[0:16])
        sqb_ap = hidF[:, 0:8, :]
        xalt = TB(None, hid.bufs[16:32])
        xalt_ap = hidF[:, 8:16, :]
        ring = [tb("ring%d" % i, [128, 4096], BF16) for i in range(RING)]
        ot = tb("ot", [128, KC, NT], BF16, KC)
        mqt = tb("mqt", [128, 4, NT], BF16, 4)
        mot = tb("mot", [128, 4, NT], BF16, 4)
        ones_f = tb("ones_f", [128, 128], F32)
        onesb = tb("onesb", [128, 128], BF16)
        vecs = tb("vecs", [128, NVEC, 8], F32)
        mkt = [tb("mkt%d" % l, [128, 4, MEM], BF16) for l in range(2)]
        mvt = [tb("mvt%d" % l, [128, 2, 512], BF16) for l in range(2)]
        normp = tb("normp", [128, NT], F32)
        normr = tb("normr", [128, NT], F32)
        psum = [TB(st.enter_context(nc.psum_tensor("ps%d" % i, [128, 512], F32)), [Buf("ps%d" % i)]) for i in range(8)]

        ctr = {"ps": 0}
        cur = {}

        def slot():
            return cur["slot"]()

        def ptb():
            return cur["pt"]()

        def bank():
            s = psum[ctr["ps"] % 8]
            ctr["ps"] += 1
            return s

        def rr(lst):
            c = [0]

            def f():
                s = lst[c[0] % len(lst)]
                c[0] += 1
                return s
            return f

        xT_b = Buf("xT")
        hs_b = [Buf("hs%d" % i) for i in range(NTILE)]
        hs1_b = [Buf("hs1_%d" % i) for i in range(NTILE)]
        kts_b = [Buf("kts%d" % i) for i in range(NTILE)]
        vas_b = [Buf("vas%d" % i) for i in range(NTILE)]
        x1s_b = [Buf("x1s%d" % i) for i in range(NTILE)]
        xbs_b = [Buf("xbs%d" % i) for i in range(NTILE)]
        xbs_pad = Buf("xbspad")
        xcs_b = [Buf("xcs%d" % i) for i in range(NTILE)]
        h1s_b = [Buf("h1s%d" % i) for i in range(NTILE)]
        out_b = Buf("out")
        wscr_b = {k: Buf("w" + str(k)) for k in plan}
        const_b = Buf("constin")

        def record(P, W, tag):
            ctr["ps"] = 0
            cast_done = set()
            cast_chain = [Buf("castchain0"), Buf("castchain1")]

            def need_cast(key):
                if key in cast_done:
                    return
                cast_done.add(key)
                d = plan[key]
                src = wsrc[d["src"]]
                if d["lsel"] is not None:
                    src = src[d["lsel"]]
                src3 = src.rearrange("(k p) n -> p k n", p=128)
                kcs, ncs = d["kcs"], d["ncs"]
                n = kcs * ncs
                P.dma(mk("dma_start", out=wscr[:, d["off"]:d["off"] + n].rearrange("p (k n) -> p k n", n=ncs),
                         in_=src3[:, d["k0"]:d["k0"] + kcs, d["c0"]:d["c0"] + ncs]),
                      reads=[const_b], writes=[wscr_b[key], cast_chain[len(cast_done) % 2]], eng="pool")
            W.need_cast = need_cast
            cast_pos = [0]

            def emit_casts(n):
                if W.seq is None:
                    return
                while n > 0 and cast_pos[0] < len(W.seq):
                    k = W.seq[cast_pos[0]]
                    cast_pos[0] += 1
                    if k not in cast_done:
                        need_cast(k)
                        n -= 1

            def mm_group(out_ap, bk, pairs, reads):
                n = len(pairs)
                for i, (l, r) in enumerate(pairs):
                    P.op("pe", mk("matmul", out_ap, lhsT=l, rhs=r, start=(i == 0), stop=(i == n - 1)),
                         reads=reads, writes=[bk.b], inc=(i == n - 1))

            def norm_part1(x_ap, xbufs, nt):
                P.op("act", mk("activation", out=sqb_ap[:, :, 0:nt], in_=x_ap, func=AF.Square),
                     reads=xbufs, writes=sqb.bufs)
                P.op("dve", mk("tensor_reduce", out=normp.t[:, 0:nt],
                               in_=sqb_ap[:, :, 0:nt].rearrange("p c t -> p t c"), axis=AX.X, op=ALU.add),
                     reads=sqb.bufs, writes=[normp.b])

            def norm_part2(nt):
                bk = bank()
                P.op("pe", mk("matmul", bk.t[:, 0:nt], lhsT=ones_f.t[:], rhs=normp.t[:, 0:nt], start=True, stop=True),
                     reads=[normp.b, ones_f.b], writes=[bk.b])
                P.op("act", mk("activation", out=normr.t[:, 0:nt], in_=bk.t[:, 0:nt], func=AF.Sqrt,
                               scale=1.0 / D, bias=EPS), reads=[bk.b], writes=[normr.b])
                P.op("dve", mk("reciprocal", out=normr.t[:, 0:nt], in_=normr.t[:, 0:nt]), reads=[normr.b], writes=[normr.b])
                return normr

            def norm_stats(x_ap, xbufs, nt):
                norm_part1(x_ap, xbufs, nt)
                return norm_part2(nt)

            def apply_norm(h_tb, x_ap, xbufs, rs, nt, gidx):
                for c in range(KC):
                    P.op("dve", mk("scalar_tensor_tensor", out=h_tb.t[:, c, 0:nt], in0=x_ap[:, c, :],
                                   scalar=vecs.t[:, gidx, c:c + 1], in1=rs.t[:, 0:nt], op0=ALU.mult, op1=ALU.mult),
                         reads=[xbufs[c], rs.b, vecs.b], writes=h_tb.bufs)

            P.op("dve", mk("memset", ones_f.t[:], 1.0), writes=[ones_f.b])
            P.op("dve", mk("memset", onesb.t[:], 1.0), writes=[onesb.b])
            P.dma(mk("dma_start", out=vecs.t[:], in_=vecs_d), reads=[const_b], writes=[vecs.b])
            emit_casts(12)

            def mem_attention(l):
                def scores(hm):
                    pl = []
                    for mb in range(2):
                        sc = bank()
                        P.op("pe", mk("matmul", sc.t[:, :], lhsT=mkt[l].t[:, hm, mb * 128:(mb + 1) * 128],
                                      rhs=mqt.t[:, hm, :], start=True, stop=True),
                             reads=[mkt[l].b, mqt.bufs[hm]], writes=[sc.b])
                        pt = ptb()
                        P.op("act", mk("activation", out=pt.t[:], in_=sc.t[:, :], func=AF.Exp, scale=128.0 ** -0.5),
                             reads=[sc.b], writes=[pt.b])
                        pl.append(pt)
                    return pl

                def finish(hm, pl):
                    num = bank()
                    mm_group(num.t[:, :], num, [(mvt[l].t[:, mb, hm * 128:(hm + 1) * 128], pl[mb].t[:]) for mb in range(2)],
                             [mvt[l].b, pl[0].b, pl[1].b])
                    den = bank()
                    mm_group(den.t[:, :], den, [(onesb.t[:], pl[mb].t[:]) for mb in range(2)],
                             [onesb.b, pl[0].b, pl[1].b])
                    r = slot()
                    P.op("dve", mk("reciprocal", out=r.t[:, 0:NT], in_=den.t[:, :]), reads=[den.b], writes=[r.b])
                    P.op("dve", mk("tensor_tensor", out=mot.t[:, hm, :], in0=num.t[:, :], in1=r.t[:, 0:NT], op=ALU.mult),
                         reads=[num.b, r.b], writes=[mot.bufs[hm]])
                prev = None
                for hm in range(4):
                    pl = scores(hm)
                    if prev is not None:
                        finish(*prev)
                    prev = (hm, pl)
                finish(*prev)

            def out_proj_mm(l, s_list):
                srcs = [(ot.t[:, c, :], ot.bufs[c]) for c in range(KC)] + [(mot.t[:, hm, :], mot.bufs[hm]) for hm in range(4)]
                out = []
                for s in s_list:
                    w = W.get(("wo", l, s))
                    w3 = w.t[:, 0:12 * 256].rearrange("p (k n) -> p k n", n=256)
                    for oo in range(2):
                        oc = 2 * s + oo
                        bk = bank()
                        mm_group(bk.t[:, :], bk, [(w3[:, kc, oo * 128:(oo + 1) * 128], srcs[kc][0]) for kc in range(12)],
                                 [w.b] + [x[1] for x in srcs])
                        out.append((oc, bk))
                return out

            def out_proj_evac(lst, res=None):
                for oc, bk in lst:
                    if res is None:
                        P.op("dve", mk("tensor_tensor", out=xt.t[:, oc, :], in0=bk.t[:, :], in1=xt.t[:, oc, :], op=ALU.add),
                             reads=[bk.b], writes=[xt.bufs[oc]])
                    else:
                        P.op("dve", mk("tensor_tensor", out=xt.t[:, oc, :], in0=bk.t[:, :], in1=res[0][:, oc, :], op=ALU.add),
                             reads=[bk.b] + list(res[1]), writes=[xt.bufs[oc]])

            def out_proj(l, res=None):
                for s in range(4):
                    out_proj_evac(out_proj_mm(l, [s]), res)

            def mlp(l, hb, hooks=()):
                hk = dict(hooks) if isinstance(hooks, dict) else {k: [f] for k, f in enumerate(hooks)}
                hc = [0]

                def H():
                    for f in hk.pop(hc[0], []):
                        f()
                    hc[0] += 1
                norm_part1(xt.t[:], xt.bufs, NT)
                H()
                rs = norm_part2(NT)
                apply_norm(hb, xt.t[:], xt.bufs, rs, NT, V_MLPG0 + l)
                H()
                for s in range(8):
                    w = W.get(("wu", l, s))
                    w3 = w.t[:].rearrange("p (k n) -> p k n", n=512)
                    for cc in range(4):
                        oc = 4 * s + cc
                        bk = bank()
                        mm_group(bk.t[:, :], bk, [(w3[:, kc, cc * 128:(cc + 1) * 128], hb.t[:, kc, :]) for kc in range(KC)],
                                 [w.b, hb.b])
                        r = slot()
                        P.op("act", mk("activation", out=r.t[:, 0:NT], in_=bk.t[:, :], func=AF.Relu), reads=[bk.b], writes=[r.b])
                        P.op("pool", mk("tensor_tensor", out=hid.t[:, oc, :], in0=r.t[:, 0:NT], in1=r.t[:, 0:NT], op=ALU.mult),
                             reads=[r.b], writes=[hid.bufs[oc]])
                    H()
                for oc in range(8):
                    w = W.get(("wd", l, oc))
                    w3 = w.t[:].rearrange("p (k n) -> p k n", n=128)
                    bk = bank()
                    mm_group(bk.t[:, :], bk, [(w3[:, kc, :], hid.t[:, kc, :]) for kc in range(32)], [w.b] + hid.bufs)
                    P.op("dve", mk("tensor_tensor", out=xt.t[:, oc, :], in0=bk.t[:, :], in1=xt.t[:, oc, :], op=ALU.add),
                         reads=[bk.b], writes=[xt.bufs[oc]])
                    H()
                for k in sorted(hk):
                    for f in hk[k]:
                        f()

            with contextlib.ExitStack() as st0:
                def tb0(name, shape, dt, nb=1):
                    return TB(st0.enter_context(nc.sbuf_tensor("S_" + name + tag, shape, dt)),
                              [Buf("%s.%d" % (name, i)) for i in range(nb)])

                slots0 = [tb0("slot%d" % i, [128, NT], F32) for i in range(8)]
                pts0 = [tb0("pt%d" % i, [128, NT], BF16) for i in range(12)]
                cur["slot"] = rr(slots0)
                cur["pt"] = rr(pts0)
                kth = tb0("kth", [128, 4, 6 * 128], BF16)
                vah = tb0("vah", [128, 6, 4, 128], BF16)
                ktile = tb0("ktile", [128, 4, NT], BF16)
                vtile = tb0("vtile", [128, 4, 4, 128], BF16)
                posi = tb0("posi", [128, NT], I32)
                ctab = tb0("ctab", [128, NT], F32)
                stab = tb0("stab", [128, NT], F32)
                qt = tb0("qt", [128, KC, NT], BF16, KC)
                cv = tb0("cv", [128, 2], F32)
                tri = tb0("tri", [128, 2, 128], BF16)
                sink2 = tb0("sink2", [2, 16], F32)
                es2 = tb0("es2", [2, 16], F32)
                hi2 = tb0("hi2", [2, 16], BF16)
                hif = tb0("hif", [2, 16], F32)
                cm = tb0("cm", [2, 2], F32)
                esr = tb0("esr", [128, 16, 128], BF16)
                selE = tb0("selE", [128, 128], BF16)
                selO = tb0("selO", [128, 128], BF16)

                P.dma(mk("dma_start", out=cv.t[:], in_=cv_d), reads=[const_b], writes=[cv.b])
                P.dma(mk("dma_start", out=tri.t[:], in_=tri_d), reads=[const_b], writes=[tri.b], eng="pool")
                P.dma(mk("dma_start", out=cm.t[:], in_=cm_d), reads=[const_b], writes=[cm.b])
                P.dma(mk("dma_start", out=sink2.t[:], in_=sinks_d.partition_broadcast(2)), reads=[const_b], writes=[sink2.b])
                P.op("act", mk("activation", out=es2.t[:], in_=sink2.t[:], func=AF.Exp), reads=[sink2.b], writes=[es2.b])
                P.op("dve", mk("tensor_copy", out=hi2.t[:], in_=es2.t[:]), reads=[es2.b], writes=[hi2.b])
                P.op("dve", mk("tensor_copy", out=hif.t[:], in_=hi2.t[:]), reads=[hi2.b], writes=[hif.b])
                P.op("dve", mk("tensor_tensor", out=es2.t[:], in0=es2.t[:], in1=hif.t[:], op=ALU.subtract),
                     reads=[es2.b, hif.b], writes=[es2.b])
                P.op("dve", mk("tensor_scalar", out=hif.t[:], in0=hif.t[:], scalar1=cm.t[:, 0:1], scalar2=None, op0=ALU.mult),
                     reads=[hif.b, cm.b], writes=[hif.b])
                P.op("dve", mk("scalar_tensor_tensor", out=hif.t[:], in0=es2.t[:], scalar=cm.t[:, 1:2], in1=hif.t[:],
                               op0=ALU.mult, op1=ALU.add), reads=[es2.b, hif.b, cm.b], writes=[hif.b])
                P.op("dve", mk("memset", esr.t[:], 0.0), writes=[esr.b])
                P.op("dve", mk("tensor_copy", out=esr.t[0:2, :, :], in_=bc_last(hif.t[:], 128)), reads=[hif.b], writes=[esr.b])
                P.op("dve", mk("memset", ktile.t[:], 0.0), writes=[ktile.b])
                P.op("dve", mk("memset", selE.t[:], 0.0), writes=[selE.b])
                P.op("dve", mk("memset", selE.t[0:2, 64:128], 1.0), writes=[selE.b])
                P.op("dve", mk("memset", selO.t[:], 0.0), writes=[selO.b])
                P.op("dve", mk("memset", selO.t[0:2, 0:64], 1.0), writes=[selO.b])
                P.op("dve", mk("memset", vtile.t[:], 1.0), writes=[vtile.b])

                memx = xt.t[:, :, 0:MEM]
                P.dma(mk("dma_start", out=memx, in_=memT3), reads=[const_b], writes=xt.bufs)
                rs = norm_stats(memx, xt.bufs, MEM)
                memh = hb2[0]
                apply_norm(memh, memx, xt.bufs, rs, MEM, V_MEMG)
                for l in range(2):
                    w = W.get(("mkv", l, 0))
                    w3 = w.t[:].rearrange("p (k n) -> p k n", n=512)
                    for hm in range(4):
                        bk = bank()
                        mm_group(bk.t[:, 0:MEM], bk, [(w3[:, kc, hm * 128:(hm + 1) * 128], memh.t[:, kc, 0:MEM]) for kc in range(KC)],
                                 [w.b, memh.b])
                        P.op("act", mk("copy", out=mkt[l].t[:, hm, :], in_=bk.t[:, 0:MEM]), reads=[bk.b], writes=[mkt[l].b])
                    w = W.get(("mkv", l, 1))
                    w3 = w.t[:].rearrange("p (k n) -> p k n", n=512)
                    for mb in range(2):
                        bk = bank()
                        mm_group(bk.t[:, :], bk, [(memh.t[:, kc, mb * 128:(mb + 1) * 128], w3[:, kc, :]) for kc in range(KC)],
                                 [w.b, memh.b])
                        P.op("act", mk("copy", out=mvt[l].t[:, mb, :], in_=bk.t[:, :]), reads=[bk.b], writes=[mvt[l].b])

                swapmask = list(range(32))
                for i in range(8):
                    swapmask[i], swapmask[8 + i] = 8 + i, i

                def rope_tables(t0):
                    P.dma(mk("dma_start", out=posi.t[:], in_=pos[0:1, t0:t0 + NT].partition_broadcast(128)),
                          reads=[const_b], writes=[posi.b])
                    pf = slot()
                    P.op("dve", mk("tensor_copy", out=pf.t[:, 0:NT], in_=posi.t[:]), reads=[posi.b], writes=[pf.b])
                    for col, offv, tab in ((0, 0.75, ctab), (1, 0.5, stab)):
                        u = slot()
                        P.op("dve", mk("tensor_scalar", out=u.t[:, 0:NT], in0=pf.t[:, 0:NT], scalar1=cv.t[:, col:col + 1],
                                       scalar2=offv, op0=ALU.mult, op1=ALU.add), reads=[pf.b, cv.b], writes=[u.b])
                        kiv = posi.t[:]
                        P.op("dve", mk("tensor_copy", out=kiv, in_=u.t[:, 0:NT]), reads=[u.b], writes=[posi.b])
                        kf = slot()
                        P.op("dve", mk("tensor_copy", out=kf.t[:, 0:NT], in_=kiv), reads=[posi.b], writes=[kf.b])
                        P.op("dve", mk("tensor_tensor", out=u.t[:, 0:NT], in0=u.t[:, 0:NT], in1=kf.t[:, 0:NT], op=ALU.subtract),
                             reads=[u.b, kf.b], writes=[u.b])
                        P.op("dve", mk("scalar_tensor_tensor", out=kf.t[:, 0:NT], in0=u.t[:, 0:NT], scalar=0.0, in1=u.t[:, 0:NT],
                                       op0=ALU.is_lt, op1=ALU.add), reads=[u.b], writes=[kf.b])
                        P.op("act", mk("activation", out=tab.t[:], in_=kf.t[:, 0:NT], func=AF.Sin,
                                       scale=2.0 * math.pi, bias=-math.pi), reads=[kf.b], writes=[tab.b])

                def rope(bk, out_ap, out_buf, add_eng, split=None):
                    sw = slot()
                    P.op("dve", mk("stream_shuffle", out=sw.t[:, 0:NT], in_=bk.t[:, :], mask=swapmask),
                         reads=[bk.b], writes=[sw.b])
                    P.op("dve", mk("tensor_tensor", out=sw.t[:, 0:NT], in0=sw.t[:, 0:NT], in1=stab.t[:], op=ALU.mult),
                         reads=[sw.b, stab.b], writes=[sw.b])
                    t1 = slot()
                    P.op("dve", mk("tensor_tensor", out=t1.t[:, 0:NT], in0=bk.t[:, :], in1=ctab.t[:], op=ALU.mult),
                         reads=[bk.b, ctab.b], writes=[t1.b])
                    if split is None:
                        P.op(add_eng, mk("tensor_tensor", out=out_ap, in0=t1.t[:, 0:NT], in1=sw.t[:, 0:NT], op=ALU.add),
                             reads=[t1.b, sw.b], writes=[out_buf])
                    else:
                        for (r0_, r1_, o_ap) in split:
                            P.op(add_eng, mk("tensor_tensor", out=o_ap, in0=t1.t[r0_:r1_, 0:NT], in1=sw.t[r0_:r1_, 0:NT], op=ALU.add),
                                 reads=[t1.b, sw.b], writes=[out_buf])

                xA = [(xt.t[:], xt.bufs), (xalt_ap, xalt.bufs)]

                def phA_load(i):
                    t0 = i * NT
                    x_ap, x_bufs = xA[i % 2]
                    P.dma(mk("dma_start", out=x_ap, in_=xT3[:, :, t0:t0 + NT]), reads=[xT_b], writes=x_bufs)

                def phA_front1(i):
                    x_ap, x_bufs = xA[i % 2]
                    norm_part1(x_ap, x_bufs, NT)

                def phA_front(i):
                    t0 = i * NT
                    x_ap, x_bufs = xA[i % 2]
                    hb = hb2[i % 2]
                    rs = norm_part2(NT)
                    apply_norm(hb, x_ap, x_bufs, rs, NT, V_MIXG0)
                    P.dma(mk("dma_start", out=hs[:, :, t0:t0 + NT], in_=hb.t[:]), reads=[hb.b], writes=[hs_b[i]])

                wkv = W.get(("kv0", 0))

                def phA_back(i):
                    t0 = i * NT
                    hb = hb2[i % 2]
                    w = wkv
                    w3 = w.t[:].rearrange("p (k n) -> p k n", n=512)
                    for p in range(2):
                        bk = bank()
                        mm_group(bk.t[:, :], bk, [(w3[:, kc, p * 128:(p + 1) * 128], hb.t[:, kc, :]) for kc in range(KC)],
                                 [w.b, hb.b])
                        rope(bk, None, ktile.b, "dve", split=[(0, 64, ktile.t[0:64, 2 * p, :]), (64, 128, ktile.t[64:128, 2 * p + 1, :])])
                    P.dma(mk("dma_start", out=kts[:, :, t0:t0 + NT], in_=ktile.t[:]), reads=[ktile.b], writes=[kts_b[i]])
                    for b in range(4):
                        bk = bank()
                        mm_group(bk.t[:, 0:256], bk, [(hb.t[:, kc, b * 128:(b + 1) * 128], w3[:, kc, 256:512]) for kc in range(KC)],
                                 [w.b, hb.b])
                        bk3 = bk.t[:, 0:256].rearrange("p (h d) -> p h d", d=64)
                        P.op("act", mk("copy", out=vtile.t[:, b, 0::2, 0:64], in_=bk3[:, 0::2, :]), reads=[bk.b], writes=[vtile.b])
                        P.op("act", mk("copy", out=vtile.t[:, b, 1::2, 64:128], in_=bk3[:, 1::2, :]), reads=[bk.b], writes=[vtile.b])
                    P.dma(mk("dma_start", out=vas[:, 4 * i:4 * i + 4, :], in_=vtile.t[:].rearrange("p b h d -> p b (h d)")),
                          reads=[vtile.b], writes=[vas_b[i]])

                phA_load(0)
                phA_load(1)
                phA_front1(0)
                phA_front(0)
                rope_tables(0)
                for i in range(NTILE):
                    if i + 2 < NTILE:
                        phA_load(i + 2)
                    if i + 1 < NTILE:
                        phA_front1(i + 1)
                    phA_back(i)
                    if i + 1 < NTILE:
                        phA_front(i + 1)
                    if i + 1 < NTILE:
                        rope_tables((i + 1) * NT)
                    emit_casts(1)

                def attention_pieces(i):
                    b0 = 4 * i
                    groups = [(b, p) for b in range(4) for p in range(2)]

                    def scores(b, p):
                        res = []
                        for e in range(2):
                            r0, r1 = e * 64, (e + 1) * 64
                            js = [j for j in (-1, 0, 1) if 0 <= b0 + b + j < NBLK]
                            lst = []
                            for j in js:
                                hbk = (b0 + b + j) - (b0 - 1)
                                sc = bank()
                                P.op("pe", mk("matmul", sc.t[:, :], lhsT=kth.t[:, 2 * p + e, hbk * 128:(hbk + 1) * 128],
                                              rhs=qt.t[:, 4 * p:4 * p + 4, b * 128:(b + 1) * 128], start=True, stop=True),
                                     reads=[kth.b] + qt.bufs[4 * p:4 * p + 4], writes=[sc.b])
                                pt = ptb()
                                P.op("act", mk("activation", out=pt.t[:], in_=sc.t[:, :], func=AF.Exp, scale=0.125),
                                     reads=[sc.b], writes=[pt.b])
                                if j != 0:
                                    jj = 0 if j < 0 else 1
                                    pt3 = pt.t[:].rearrange("p (g q) -> p g q", q=128)
                                    P.op("pool", mk("tensor_tensor", out=pt3, in0=pt3, in1=bc_mid(tri.t[:, jj, :], 4), op=ALU.mult),
                                         reads=[pt.b, tri.b], writes=[pt.b])
                                lst.append((hbk, pt))
                            res.append(lst)
                        return res

                    def finish(b, p, res):
                        pvs = []
                        for e in range(2):
                            hk = 2 * p + e
                            lst = res[e]
                            pv = bank()
                            sel = selE if e == 0 else selO
                            P.op("pe", mk("matmul", pv.t[:, :], lhsT=sel.t[:, :], rhs=esr.t[:, 4 * hk:4 * hk + 4, :], start=True, stop=False),
                                 reads=[sel.b, esr.b], writes=[pv.b], inc=False)
                            for idx, (hbk, pt) in enumerate(lst):
                                P.op("pe", mk("matmul", pv.t[:, :], lhsT=vah.t[:, hbk, hk, :], rhs=pt.t[:],
                                              start=False, stop=(idx == len(lst) - 1)),
                                     reads=[vah.b, pt.b], writes=[pv.b], inc=(idx == len(lst) - 1))
                            pvs.append(pv)
                        dd = slot()
                        P.op("act", mk("copy", out=dd.t[0:64, 0:NT], in_=pvs[0].t[64:128, :]), reads=[pvs[0].b], writes=[dd.b])
                        P.op("act", mk("copy", out=dd.t[64:128, 0:NT], in_=pvs[1].t[0:64, :]), reads=[pvs[1].b], writes=[dd.b])
                        P.op("dve", mk("reciprocal", out=dd.t[:, 0:NT], in_=dd.t[:, 0:NT]), reads=[dd.b], writes=[dd.b])
                        for e in range(2):
                            r0, r1 = e * 64, (e + 1) * 64
                            pvn = pvs[e].t[r0:r1, :].rearrange("p (g q) -> p g q", q=128)
                            dd3 = dd.t[r0:r1, 0:NT].rearrange("p (g q) -> p g q", q=128)
                            P.op("dve", mk("tensor_tensor", out=ot.t[r0:r1, 4 * p:4 * p + 4, b * 128:(b + 1) * 128], in0=pvn, in1=dd3,
                                           op=ALU.mult), reads=[pvs[e].b, dd.b], writes=ot.bufs[4 * p:4 * p + 4])
                    stt = {"prev": None}

                    def piece(g):
                        def f():
                            if g is not None:
                                res = scores(*g)
                            if stt["prev"] is not None:
                                finish(*stt["prev"])
                            stt["prev"] = (g[0], g[1], res) if g is not None else None
                        return f
                    return [piece(g) for g in groups] + [piece(None)]

                def frontA(i):
                    t0 = i * NT
                    b0 = 4 * i
                    hb = hb2[i % 2]
                    P.dma(mk("dma_start", out=hb.t[:], in_=hs[:, :, t0:t0 + NT]), reads=[hs_b[i]], writes=[hb.b])
                    lo = max(0, b0 - 1)
                    hi = min(NBLK, b0 + 5)
                    h0 = lo - (b0 - 1)
                    nb_ = hi - lo
                    tl = sorted(set([(lo * 128) // NT, i, ((hi - 1) * 128) // NT]))
                    P.dma(mk("dma_start", out=kth.t[:, :, h0 * 128:(h0 + nb_) * 128], in_=kts[:, :, lo * 128:hi * 128]),
                          reads=[kts_b[j] for j in tl], writes=[kth.b])
                    P.dma(mk("dma_start", out=vah.t[:, h0:h0 + nb_, :, :].rearrange("p b h d -> p b (h d)"), in_=vas[:, lo:hi, :]),
                          reads=[vas_b[j] for j in tl], writes=[vah.b])
                    rope_tables(t0)
                    qslice(i, 0)

                def qslice(i, s):
                    hb = hb2[i % 2]
                    w = W.get(("q0", s))
                    w3 = w.t[:].rearrange("p (k n) -> p k n", n=512)
                    for cc in range(4):
                        c = 4 * s + cc
                        bk = bank()
                        mm_group(bk.t[:, :], bk, [(w3[:, kc, cc * 128:(cc + 1) * 128], hb.t[:, kc, :]) for kc in range(KC)],
                                 [w.b, hb.b])
                        rope(bk, qt.t[:, c, :], qt.bufs[c], "pool")

                def frontB(i):
                    hb = hb2[i % 2]
                    qslice(i, 1)
                    w = W.get(("mq0", 0))
                    w3 = w.t[:].rearrange("p (k n) -> p k n", n=512)
                    for hm in range(4):
                        bk = bank()
                        mm_group(bk.t[:, :], bk, [(w3[:, kc, hm * 128:(hm + 1) * 128], hb.t[:, kc, :]) for kc in range(KC)],
                                 [w.b, hb.b])
                        P.op("act", mk("copy", out=mqt.t[:, hm, :], in_=bk.t[:, :]), reads=[bk.b], writes=[mqt.bufs[hm]])

                def xt_load(i):
                    t0 = i * NT
                    P.dma(mk("dma_start", out=xt.t[:], in_=xT3[:, :, t0:t0 + NT]), reads=[xT_b], writes=xt.bufs)

                def l1_finish(i):
                    t0 = i * NT
                    hbl = hb2[i % 2]
                    rs = norm_part2(NT)
                    apply_norm(hbl, xt.t[:], xt.bufs, rs, NT, V_MIXG1)
                    P.dma(mk("dma_start", out=hs1[:, :, t0:t0 + NT], in_=hbl.t[:]), reads=[hbl.b], writes=[hs1_b[i]])

                def l1_xb(i):
                    t0 = i * NT
                    hbl = hb2[i % 2]
                    for s in range(2):
                        w = W.get(("xb1", s))
                        w3 = w.t[:].rearrange("p (k n) -> p k n", n=512)
                        for cc in range(4):
                            c = 4 * s + cc
                            bk = bank()
                            mm_group(bk.t[:, :], bk, [(w3[:, kc, cc * 128:(cc + 1) * 128], hbl.t[:, kc, :]) for kc in range(KC)],
                                     [w.b, hbl.b])
                            P.op("act", mk("copy", out=xalt_ap[:, c, :], in_=bk.t[:, :]), reads=[bk.b], writes=xalt.bufs)
                    P.dma(mk("dma_start", out=xbs[:, :, 2 + t0:2 + t0 + NT], in_=xalt_ap), reads=xalt.bufs, writes=[xbs_b[i]])

                frontA(0)
                frontB(0)
                for f in attention_pieces(0):
                    f()
                xt_load(0)
                mem_attention(0)
                for i in range(NTILE):
                    t0 = i * NT
                    hb = hb2[i % 2]
                    out_proj(0)
                    hooks = []
                    if i + 1 < NTILE:
                        ap_ = attention_pieces(i + 1)
                        hooks = {0: [(lambda i=i: frontA(i + 1))], 1: [(lambda i=i: frontB(i + 1))]}
                        for k_, idx_ in enumerate((3, 5, 7, 9, 10, 11, 12, 13, 14)):
                            hooks[idx_] = [ap_[k_]]
                        hooks[15] = [lambda: mem_attention(0)]
                    mlp(0, hb, hooks)
                    P.dma(mk("dma_start", out=x1s[:, :, t0:t0 + NT], in_=xt.t[:]), reads=xt.bufs, writes=[x1s_b[i]])
                    norm_part1(xt.t[:], xt.bufs, NT)
                    l1_finish(i)
                    if i + 1 < NTILE:
                        xt_load(i + 1)
                    l1_xb(i)
                    emit_casts(3)
            P.barrier()

            with contextlib.ExitStack() as st1:
                def tb1(name, shape, dt, nb=1):
                    return TB(st1.enter_context(nc.sbuf_tensor("S_" + name + tag, shape, dt)),
                              [Buf("%s.%d" % (name, i)) for i in range(nb)])

                lslots = [tb1("lslot%d" % i, [128, NT], F32) for i in range(16)]
                sslots = [tb1("sslot%d" % i, [128, NT], F32) for i in range(8)]
                xbhs = [tb1("xbh%d" % i, [128, NT + 4], F32) for i in range(4)]
                xbhn = rr(xbhs)
                pts1 = [tb1("ptl%d" % i, [128, NT], BF16) for i in range(4)]
                lslot = rr(lslots)
                cur["slot"] = rr(sslots)
                cur["pt"] = rr(pts1)
                wgate = tb1("wgate", [128, 4096], BF16)
                wg5 = wgate.t[:].rearrange("p (x d n e) -> p x d n e", x=2, d=2, n=8)
                dvec = tb1("dvec", [128, 6, 8], F32)
                dtmp = [tb1("dtmp%d" % i, [128, 2, 8], F32) for i in range(3)]
                carry = [tb1("carry%d" % i, [128, 8], F32) for i in range(2)]
                xcb3 = [tb1("xcb%d" % i, [128, NT], BF16) for i in range(4)]
                zpad = tb1("zpad", [128, 8, 2], F32)
                xcbn = rr(xcb3)

                for x_, src in enumerate((wa_d, wx_d)):
                    for d_ in range(2):
                        P.dma(mk("dma_start", out=wg5[:, x_, d_, :, :], in_=src[d_].rearrange("n k e -> k n e")),
                              reads=[const_b], writes=[wgate.b], eng="pool")
                lam = vecs.t[:, V_LAM:V_LAM + 2, :]
                e_, dn, z_ = dtmp
                P.op("act", mk("activation", out=e_.t[:], in_=lam, func=AF.Exp, scale=-1.0), reads=[vecs.b], writes=[e_.b])
                P.op("dve", mk("tensor_scalar", out=dn.t[:], in0=e_.t[:], scalar1=2.0, scalar2=None, op0=ALU.add),
                     reads=[e_.b], writes=[dn.b])
                P.op("dve", mk("reciprocal", out=dn.t[:], in_=dn.t[:]), reads=[dn.b], writes=[dn.b])
                P.op("dve", mk("tensor_tensor", out=z_.t[:], in0=e_.t[:], in1=dn.t[:], op=ALU.mult), reads=[e_.b, dn.b], writes=[z_.b])
                P.op("dve", mk("tensor_tensor", out=dn.t[:], in0=z_.t[:], in1=z_.t[:], op=ALU.mult), reads=[z_.b], writes=[dn.b])
                P.op("dve", mk("tensor_scalar", out=dn.t[:], in0=dn.t[:], scalar1=1.0 / 3.0, scalar2=1.0, op0=ALU.mult, op1=ALU.add),
                     reads=[dn.b], writes=[dn.b])
                P.op("dve", mk("tensor_tensor", out=dn.t[:], in0=dn.t[:], in1=z_.t[:], op=ALU.mult), reads=[dn.b, z_.b], writes=[dn.b])
                P.op("dve", mk("tensor_scalar", out=dvec.t[:, 0:2, :], in0=dn.t[:], scalar1=-8.0, scalar2=None, op0=ALU.mult),
                     reads=[dn.b], writes=[dvec.b])
                P.op("dve", mk("tensor_scalar", out=dvec.t[:, 2:4, :], in0=vecs.t[:, V_BA:V_BA + 2, :], scalar1=0.5, scalar2=None,
                               op0=ALU.mult), reads=[vecs.b], writes=[dvec.b])
                P.op("dve", mk("tensor_scalar", out=dvec.t[:, 4:6, :], in0=vecs.t[:, V_BX:V_BX + 2, :], scalar1=0.5, scalar2=None,
                               op0=ALU.mult), reads=[vecs.b], writes=[dvec.b])
                for cb_ in carry:
                    P.op("dve", mk("memset", cb_.t[:], 0.0), writes=[cb_.b])
                P.op("dve", mk("memset", zpad.t[:], 0.0), writes=[zpad.b])
                P.dma(mk("dma_start", out=xbs[:, :, 0:2], in_=zpad.t[:]), reads=[zpad.b], writes=[xbs_pad])
                P.dma(mk("dma_start", out=xbs[:, :, NTOK + 2:NTOK + 4], in_=zpad.t[:]), reads=[zpad.b], writes=[xbs_pad])

                def lru_s1a(d_, c, xc):
                    xcb = xcbn()
                    P.op("act", mk("copy", out=xcb.t[:], in_=xc.t[:, 0:NT]), reads=[xc.b], writes=[xcb.b])
                    pa = bank()
                    P.op("pe", mk("matmul", pa.t[:, :], lhsT=wg5[:, 0, d_, c, :], rhs=xcb.t[:], start=True, stop=True),
                         reads=[wgate.b, xcb.b], writes=[pa.b])
                    px = bank()
                    P.op("pe", mk("matmul", px.t[:, :], lhsT=wg5[:, 1, d_, c, :], rhs=xcb.t[:], start=True, stop=True),
                         reads=[wgate.b, xcb.b], writes=[px.b])
                    return (pa, px)

                def lru_s1b(d_, c, xc, pp):
                    pa, px = pp
                    ta = lslot()
                    P.op("act", mk("activation", out=ta.t[:, 0:NT], in_=pa.t[:, :], func=AF.Tanh, scale=0.5,
                                   bias=dvec.t[:, 2 + d_, c:c + 1]), reads=[pa.b, dvec.b], writes=[ta.b])
                    tx = lslot()
                    P.op("act", mk("activation", out=tx.t[:, 0:NT], in_=px.t[:, :], func=AF.Tanh, scale=0.5,
                                   bias=dvec.t[:, 4 + d_, c:c + 1]), reads=[px.b, dvec.b], writes=[tx.b])
                    P.op("act", mk("activation", out=ta.t[:, 0:NT], in_=ta.t[:, 0:NT], func=AF.Exp,
                                   scale=dvec.t[:, d_, c:c + 1], bias=dvec.t[:, d_, c:c + 1]), reads=[ta.b, dvec.b], writes=[ta.b])
                    sq = lslot()
                    P.op("pool", mk("tensor_tensor", out=sq.t[:, 0:NT], in0=ta.t[:, 0:NT], in1=ta.t[:, 0:NT], op=ALU.mult),
                         reads=[ta.b], writes=[sq.b])
                    P.op("act", mk("activation", out=sq.t[:, 0:NT], in_=sq.t[:, 0:NT], func=AF.Relu, scale=-1.0, bias=CLAMP_C),
                         reads=[sq.b], writes=[sq.b])
                    P.op("dve", mk("scalar_tensor_tensor", out=tx.t[:, 0:NT], in0=tx.t[:, 0:NT], scalar=1.0, in1=xc.t[:, 0:NT],
                                   op0=ALU.add, op1=ALU.mult), reads=[tx.b, xc.b], writes=[tx.b])
                    return (ta, tx, sq)

                def lru_sqrt(st_):
                    sq = st_[2]
                    P.op("act", mk("activation", out=sq.t[:, 0:NT], in_=sq.t[:, 0:NT], func=AF.Sqrt, scale=1.0, bias=1.0 - CLAMP_C),
                         reads=[sq.b], writes=[sq.b])

                def lru_s2(c, st_, reverse, cb_, do_sqrt=True):
                    ta, tx, sq = st_
                    if do_sqrt:
                        lru_sqrt(st_)
                    P.op("dve", mk("scalar_tensor_tensor", out=tx.t[:, 0:NT], in0=tx.t[:, 0:NT], scalar=0.5, in1=sq.t[:, 0:NT],
                                   op0=ALU.mult, op1=ALU.mult), reads=[tx.b, sq.b], writes=[tx.b])
                    hh = slot()
                    if not reverse:
                        P.op("dve", mk("tensor_tensor_scan", out=hh.t[:, 0:NT], data0=ta.t[:, 0:NT], data1=tx.t[:, 0:NT],
                                       initial=cb_.t[:, c:c + 1], op0=ALU.mult, op1=ALU.add),
                             reads=[ta.b, tx.b, cb_.b], writes=[hh.b])
                        P.op("dve", mk("tensor_copy", out=cb_.t[:, c:c + 1], in_=hh.t[:, NT - 1:NT]), reads=[hh.b], writes=[cb_.b])
                    else:
                        P.op("dve", mk("tensor_tensor_scan", out=hh.t[:, 0:NT][:, ::-1], data0=ta.t[:, 0:NT][:, ::-1],
                                       data1=tx.t[:, 0:NT][:, ::-1], initial=cb_.t[:, c:c + 1], op0=ALU.mult, op1=ALU.add),
                             reads=[ta.b, tx.b, cb_.b], writes=[hh.b])
                        P.op("dve", mk("tensor_copy", out=cb_.t[:, c:c + 1], in_=hh.t[:, 0:1]), reads=[hh.b], writes=[cb_.b])
                    return hh

                def p1_load(i, c):
                    t0 = i * NT
                    tl = sorted(set([max(0, i - 1), i, min(NTILE - 1, i + 1)]))
                    xbh = xbhn()
                    P.dma(mk("dma_start", out=xbh.t[:, 0:NT + 4], in_=xbs[:, c, t0:t0 + NT + 4]),
                          reads=[xbs_b[j] for j in tl] + [xbs_pad], writes=[xbh.b])
                    return xbh

                def p1_s1a(i, c, xbh):
                    t0 = i * NT
                    xc = slot()
                    P.op("pool", mk("tensor_scalar", out=xc.t[:, 0:NT], in0=xbh.t[:, 1:1 + NT],
                                    scalar1=vecs.t[:, V_CONVW, c:c + 1], scalar2=vecs.t[:, V_CONVB, c:c + 1],
                                    op0=ALU.mult, op1=ALU.add), reads=[xbh.b, vecs.b], writes=[xc.b])
                    for tap in range(1, 4):
                        P.op("dve", mk("scalar_tensor_tensor", out=xc.t[:, 0:NT], in0=xbh.t[:, 1 + tap:1 + tap + NT],
                                       scalar=vecs.t[:, V_CONVW + tap, c:c + 1], in1=xc.t[:, 0:NT],
                                       op0=ALU.mult, op1=ALU.add), reads=[xbh.b, xc.b, vecs.b], writes=[xc.b])
                    P.dma(mk("dma_start", out=xcs[:, c, t0:t0 + NT], in_=xc.t[:, 0:NT]), reads=[xc.b], writes=[xcs_b[i]])
                    return (xc, lru_s1a(0, c, xc))

                def p1_s2(i, c, st_):
                    t0 = i * NT
                    hh = lru_s2(c, st_, False, carry[0], do_sqrt=False)
                    P.dma(mk("dma_start", out=h1s[:, c, t0:t0 + NT], in_=hh.t[:, 0:NT]), reads=[hh.b], writes=[h1s_b[i]])

                seq1 = [(i, c) for i in range(NTILE) for c in range(KC)]
                pairs1 = [seq1[n_:n_ + 2] for n_ in range(0, len(seq1), 2)]
                loaded = [p1_load(i, c) for (i, c) in pairs1[0]]
                pend = []
                for j, pr in enumerate(pairs1):
                    nxt = [p1_load(i, c) for (i, c) in pairs1[j + 1]] if j + 1 < len(pairs1) else []
                    for (i, c, st_) in pend:
                        lru_sqrt(st_)
                    sa = [(i, c, p1_s1a(i, c, xb_)) for (i, c), xb_ in zip(pr, loaded)]
                    for (i, c, st_) in pend:
                        p1_s2(i, c, st_)
                    pend = [(i, c, lru_s1b(0, c, xc, pp)) for (i, c, (xc, pp)) in sa]
                    loaded = nxt
                for (i, c, st_) in pend:
                    lru_sqrt(st_)
                for (i, c, st_) in pend:
                    p1_s2(i, c, st_)

                def p2_load(i, c):
                    t0 = i * NT
                    xc = xbhn()
                    P.dma(mk("dma_start", out=xc.t[:, 0:NT], in_=xcs[:, c, t0:t0 + NT]), reads=[xcs_b[i]], writes=[xc.b])
                    return xc

                def p2_s1a(i, c, xc):
                    t0 = i * NT
                    h1 = lslot()
                    P.dma(mk("dma_start", out=h1.t[:, 0:NT], in_=h1s[:, c, t0:t0 + NT]), reads=[h1s_b[i]], writes=[h1.b])
                    return (xc, h1, lru_s1a(1, c, xc))

                def p2_s2(i, c, st_, hb, wg_):
                    hh = lru_s2(c, st_[0:3], True, carry[1])
                    h1 = st_[3]
                    wgt, wgt3 = wg_
                    cc = c % 4
                    bk = bank()
                    mm_group(bk.t[:, :], bk, [(wgt3[:, kc, cc * 128:(cc + 1) * 128], hb.t[:, kc, :]) for kc in range(KC)],
                             [wgt.b, hb.b])
                    gg = slot()
                    P.op("act", mk("activation", out=gg.t[:, 0:NT], in_=bk.t[:, :], func=AF.Gelu_apprx_tanh),
                         reads=[bk.b], writes=[gg.b])
                    P.op("pool", mk("tensor_tensor", out=h1.t[:, 0:NT], in0=h1.t[:, 0:NT], in1=hh.t[:, 0:NT], op=ALU.add),
                         reads=[h1.b, hh.b], writes=[h1.b])
                    P.op("pool", mk("tensor_tensor", out=ot.t[:, c, :], in0=h1.t[:, 0:NT], in1=gg.t[:, 0:NT], op=ALU.mult),
                         reads=[h1.b, gg.b], writes=[ot.bufs[c]])

                def p2_prefetch(i):
                    t0 = i * NT
                    hb = hb2[i % 2]
                    P.dma(mk("dma_start", out=hb.t[:], in_=hs1[:, :, t0:t0 + NT]), reads=[hs1_b[i]], writes=[hb.b])
                    return [p2_load(i, c) for c in (0, 1)]

                order2 = list(range(NTILE - 1, -1, -1))

                def lru_pieces(i, first_ld):
                    hb = hb2[i % 2]
                    stt = {"loaded": first_ld, "pend": [], "wg": None}

                    def s1(n_):
                        def f():
                            nxt = [p2_load(i, c) for c in (n_ + 2, n_ + 3)] if n_ + 2 < KC else []
                            sa = [(c, p2_s1a(i, c, ld)) for c, ld in zip((n_, n_ + 1), stt["loaded"])]
                            stt["new"] = [(c, lru_s1b(1, c, xc, pp) + (h1,)) for (c, (xc, h1, pp)) in sa]
                            stt["loaded"] = nxt
                        return f

                    def s2():
                        def f():
                            for (c, st_) in stt["pend"]:
                                if c % 2 == 0:
                                    wgt = W.get(("gate1", c // 4))
                                    stt["wg"] = (wgt, wgt.t[:].rearrange("p (k n) -> p k n", n=512))
                                p2_s2(i, c, st_, hb, stt["wg"])
                            stt["pend"] = stt.pop("new", [])
                        return f
                    pcs = []
                    for n_ in range(0, KC, 2):
                        pcs.append(s1(n_))
                        pcs.append(s2())
                    pcs.append(s2())
                    return pcs

                def mq_proj(i):
                    hb = hb2[i % 2]
                    w = W.get(("mq1", 0))
                    w3 = w.t[:].rearrange("p (k n) -> p k n", n=512)
                    for hm in range(4):
                        bk = bank()
                        mm_group(bk.t[:, :], bk, [(w3[:, kc, hm * 128:(hm + 1) * 128], hb.t[:, kc, :]) for kc in range(KC)],
                                 [w.b, hb.b])
                        P.op("act", mk("copy", out=mqt.t[:, hm, :], in_=bk.t[:, :]), reads=[bk.b], writes=[mqt.bufs[hm]])

                def x1_load(i):
                    t0 = i * NT
                    P.dma(mk("dma_start", out=xalt_ap, in_=x1s[:, :, t0:t0 + NT]), reads=[x1s_b[i]], writes=xalt.bufs)

                first_ld = p2_prefetch(order2[0])
                for f in lru_pieces(order2[0], first_ld):
                    f()
                mq_proj(order2[0])
                x1_load(order2[0])
                mem_attention(1)
                held = None
                for oi, i in enumerate(order2):
                    t0 = i * NT
                    hb = hb2[i % 2]
                    if held is None:
                        out_proj(1, res=(xalt_ap, xalt.bufs))
                    else:
                        out_proj_evac(held, res=(xalt_ap, xalt.bufs))
                        out_proj_evac(out_proj_mm(1, [3]), res=(xalt_ap, xalt.bufs))
                    hooks = {}
                    nx = None
                    if oi + 1 < len(order2):
                        nx = order2[oi + 1]
                        fl = p2_prefetch(nx)
                        lp_ = lru_pieces(nx, fl)
                        hooks = {0: [(lambda nx=nx: mq_proj(nx))]}
                        for k_, idx_ in enumerate((2, 4, 6, 8, 10, 11, 12, 13, 14)):
                            hooks[idx_] = [lp_[k_]]
                        hooks[15] = [lambda: mem_attention(1)]
                    mlp(1, hb, hooks)
                    held = None
                    if nx is not None:
                        x1_load(nx)
                    norm_part1(xt.t[:], xt.bufs, NT)
                    if nx is not None:
                        held = out_proj_mm(1, [0, 1, 2])
                    rs = norm_part2(NT)
                    for c in range(KC):
                        P.op("dve", mk("scalar_tensor_tensor", out=sqb_ap[:, c, :], in0=xt.t[:, c, :],
                                       scalar=vecs.t[:, V_FING, c:c + 1], in1=rs.t[:, 0:NT], op0=ALU.mult, op1=ALU.mult),
                             reads=[xt.bufs[c], rs.b, vecs.b], writes=sqb.bufs)
                    P.dma(mk("dma_start", out=outT3[:, :, t0:t0 + NT], in_=sqb_ap), reads=sqb.bufs, writes=[out_b])
            P.barrier()

        Pd = Prog()
        Wd = WRing(Pd, ring, wscr, wscr_b, plan, None, None)
        record(Pd, Wd, "_d")
        seq = list(Wd.rec)
        for o in Buf.ALL:
            o.w = None
            o.r = []
        P = Prog()
        W = WRing(P, ring, wscr, wscr_b, plan, seq, None)
        record(P, W, "_r")
        print("recorded ops:", P.nops, {e: len(q) for e, q in P.q.items()}, "epochs", P.epoch, flush=True)
        P.emit(nc)
    return nc


_QPERM = None


def _qperm():
    cols = []
    for p in range(2):
        for g in range(4):
            for hd in (8 * p + g, 8 * p + 4 + g):
                cols.extend(range(hd * 64, (hd + 1) * 64))
    return np.array(cols, dtype=np.int64)


def _vec128(v):
    return np.ascontiguousarray(np.asarray(v, dtype=np.float32).reshape(8, 128).T)


def kernel(x, mem, positions, mix_norm, mlp_norm, mem_norm, final_norm, w_mem_kv, w_out,
           w_up, w_down, attn_w_in, attn_sinks, lru_w_in, lru_conv_w, lru_conv_b,
           lru_wa, lru_ba, lru_wx, lru_bx, lru_lambda):
    x = np.asarray(x, dtype=np.float32)
    mem = np.asarray(mem, dtype=np.float32)
    positions = np.asarray(positions).astype(np.int32)
    qp = _qperm()
    w_in0 = np.asarray(attn_w_in, dtype=np.float32)[0]
    w_in0 = np.ascontiguousarray(np.concatenate([w_in0[:, qp], w_in0[:, 1024:]], axis=1))
    w_out_p = np.array(w_out, dtype=np.float32, copy=True)
    w_out_p[0, :1024] = w_out_p[0, :1024][qp]
    vec_list = [mix_norm[0], mix_norm[1], mlp_norm[0], mlp_norm[1], mem_norm, final_norm, lru_conv_b[0],
                lru_conv_w[0][0], lru_conv_w[0][1], lru_conv_w[0][2], lru_conv_w[0][3],
                lru_ba[0][0], lru_ba[0][1], lru_bx[0][0], lru_bx[0][1], lru_lambda[0][0], lru_lambda[0][1]]
    vecs = np.ascontiguousarray(np.stack([_vec128(v) for v in vec_list], axis=1))
    invf = (np.float32(500000.0) ** (-2.0 * np.arange(8, dtype=np.float32) / np.float32(16.0))).astype(np.float32)
    cv = np.zeros((128, 2), np.float32)
    for p in range(128):
        d = p % 64
        if d < 16:
            f = float(invf[d % 8]) / (2.0 * math.pi)
            cv[p, 0] = f
            cv[p, 1] = -f if d < 8 else f
    kk = np.arange(128)[:, None]
    qq = np.arange(128)[None, :]
    tri = np.stack([(kk >= qq), (kk <= qq)], axis=1).astype(np.float32)
    shared = {
        "vecs": vecs, "cv": cv, "tri": np.ascontiguousarray(tri), "cm": np.eye(2, dtype=np.float32),
        "sinks": np.ascontiguousarray(np.asarray(attn_sinks, np.float32).reshape(1, 16)),
        "w_in0": w_in0, "w_in1": np.ascontiguousarray(np.asarray(lru_w_in, np.float32)[0]),
        "w_out": w_out_p, "w_up": np.asarray(w_up, np.float32), "w_down": np.asarray(w_down, np.float32),
        "w_mem_kv": np.asarray(w_mem_kv, np.float32),
        "lru_wa": np.ascontiguousarray(np.asarray(lru_wa, np.float32)[0]),
        "lru_wx": np.ascontiguousarray(np.asarray(lru_wx, np.float32)[0]),
    }
    real = {0: 0, 1: 1, 4: 2, 5: 3} if SPREAD8 else {0: 0, 1: 1, 2: 2, 3: 3}
    zero_shared = {k: np.zeros_like(v) for k, v in shared.items()}
    zx = np.zeros((D, NTOK), np.float32)
    zm = np.zeros((D, MEM), np.float32)
    zp = np.zeros((1, NTOK), np.int32)
    in_maps = []
    for c in range(8 if SPREAD8 else 4):
        if c in real:
            b = real[c]
            m = dict(shared)
            m["xT"] = np.ascontiguousarray(x[b].T)
            m["memT"] = np.ascontiguousarray(mem[b].T)
            m["pos"] = np.ascontiguousarray(positions[b][None, :])
        else:
            m = dict(zero_shared)
            m["xT"], m["memT"], m["pos"] = zx, zm, zp
        in_maps.append(m)
    nc = build_program()
    res = run_bass_kernel_spmd(nc, in_maps, core_ids=list(range(len(in_maps))))
    out = np.empty((4, NTOK, D), np.float32)
    for c, b in real.items():
        out[b] = res.results[c]["outT"].T
    return out
```
